# Optimizing a Trainium2 kernel written in Bass

```python
import jax, jax.numpy as jnp
from jax import lax
import numpy as np

D_MODEL = 1024
BATCH = 2
SEQ = 8192
DEPTH = 4
DEC_BATCH = 4
DEC_SEQ = 8192
PAST_LEN = 128

HEAD_DIM = 64
GRID_W = 64
NA_HEADS = 4
WIN_R = 8
WIN_C = 16
SG_GROUPS = 4
SG_GROUP_DIM = 64
SG_CHUNK = 128
GQA_Q_HEADS = 8
GQA_KV_HEADS = 2
Q_BLOCK = 128
ROPE_THETA = 10000.0
D_FF = 2816
EPS = 1e-6

NA_WIDTH = NA_HEADS * HEAD_DIM
SG_WIDTH = SG_GROUPS * SG_GROUP_DIM
GQA_Q_WIDTH = GQA_Q_HEADS * HEAD_DIM
GQA_KV_WIDTH = GQA_KV_HEADS * HEAD_DIM
D_MIX = NA_WIDTH + SG_WIDTH + GQA_Q_WIDTH
IN_SIZES = (NA_WIDTH, NA_WIDTH, NA_WIDTH, SG_WIDTH, SG_WIDTH, GQA_Q_WIDTH, GQA_KV_WIDTH, GQA_KV_WIDTH)
D_IN = sum(IN_SIZES)
IN_SPLITS = tuple(int(c) for c in np.cumsum(IN_SIZES)[:-1])

kernel_name = "hybrid_bidir_encoder_na_gmlp_gqa"


def rms_norm(x, g):
    xf = x.astype(jnp.float32)
    y = xf * lax.rsqrt(jnp.mean(xf * xf, axis=-1, keepdims=True) + EPS)
    return (y * g.astype(jnp.float32)).astype(x.dtype)


def swiglu_ffn(x, w_gate, w_up, w_down):
    return (jax.nn.silu(x @ w_gate) * (x @ w_up)) @ w_down


def neighbourhood_attention(q, k, v, rpb):
    b, s, h, d = q.shape
    rows = s // GRID_W
    wr = min(WIN_R, rows)
    q = q.reshape(b, rows, GRID_W, h, d)
    k = k.reshape(b, rows, GRID_W, h, d)
    v = v.reshape(b, rows, GRID_W, h, d)
    row_start = jnp.clip(jnp.arange(rows) - wr // 2, 0, rows - wr)
    cols = jnp.arange(GRID_W)
    col_start = jnp.clip(cols - WIN_C // 2, 0, GRID_W - WIN_C)
    col_idx = col_start[:, None] + jnp.arange(WIN_C)
    dc = col_idx - cols[:, None] + (WIN_C - 1)
    scale = HEAD_DIM ** -0.5

    def one_row(r):
        rs = row_start[r]
        q_r = lax.dynamic_index_in_dim(q, r, axis=1, keepdims=False)
        k_rows = lax.dynamic_slice_in_dim(k, rs, wr, axis=1)
        v_rows = lax.dynamic_slice_in_dim(v, rs, wr, axis=1)
        k_g = k_rows[:, :, col_idx]
        v_g = v_rows[:, :, col_idx]
        dr = rs + jnp.arange(wr) - r + (WIN_R - 1)
        bias = rpb[:, dr][:, :, dc].transpose(0, 2, 1, 3)
        sc = jnp.einsum('bchd,bicjhd->bhcij', q_r, k_g).astype(jnp.float32) * scale
        sc = sc + bias.astype(jnp.float32)[None]
        p = jax.nn.softmax(sc.reshape(b, h, GRID_W, wr * WIN_C), axis=-1)
        p = p.reshape(b, h, GRID_W, wr, WIN_C).astype(v.dtype)
        return jnp.einsum('bhcij,bicjhd->bchd', p, v_g)

    out = lax.map(one_row, jnp.arange(rows))
    return out.transpose(1, 0, 2, 3, 4).reshape(b, s, h * d)


def spatial_gating(u, v, v_norm, w_s, b_s):
    b, s, _ = u.shape
    n = s // SG_CHUNK
    u = jax.nn.gelu(u)
    v = rms_norm(jax.nn.gelu(v).reshape(b, s, SG_GROUPS, SG_GROUP_DIM), v_norm)
    vc = v.reshape(b, n, SG_CHUNK, SG_GROUPS, SG_GROUP_DIM)
    mixed = jnp.einsum('gpq,bnqgc->bnpgc', w_s, vc) + b_s.T[:, :, None]
    return u * mixed.reshape(b, s, SG_WIDTH)


def axial_rope_tables(s):
    t = jnp.arange(s)
    row = (t // GRID_W).astype(jnp.float32)
    col = (t % GRID_W).astype(jnp.float32)
    n_freq = HEAD_DIM // 4
    inv = ROPE_THETA ** (-jnp.arange(n_freq, dtype=jnp.float32) / n_freq)
    ang = jnp.concatenate([row[:, None] * inv, col[:, None] * inv], axis=-1)
    return jnp.cos(ang), jnp.sin(ang)


def apply_rope(x, cos, sin):
    xf = x.astype(jnp.float32)
    half = HEAD_DIM // 2
    x1, x2 = xf[..., :half], xf[..., half:]
    c, s_ = cos[None, :, None, :], sin[None, :, None, :]
    return jnp.concatenate([x1 * c - x2 * s_, x2 * c + x1 * s_], axis=-1).astype(x.dtype)


def gqa_attention(q, k, v):
    b, s, _, d = q.shape
    nb = s // Q_BLOCK
    rep = GQA_Q_HEADS // GQA_KV_HEADS
    scale = HEAD_DIM ** -0.5
    qb = q.reshape(b, nb, Q_BLOCK, GQA_KV_HEADS, rep, d).transpose(1, 0, 2, 3, 4, 5)

    def one_block(q_blk):
        sc = jnp.einsum('bqgrd,bkgd->bgrqk', q_blk, k).astype(jnp.float32) * scale
        p = jax.nn.softmax(sc, axis=-1).astype(v.dtype)
        return jnp.einsum('bgrqk,bkgd->bqgrd', p, v)

    out = lax.map(one_block, qb)
    return out.transpose(1, 0, 2, 3, 4, 5).reshape(b, s, GQA_Q_HEADS * d)


def trunk(x, ffn1_norm, ffn1_w_gate, ffn1_w_up, ffn1_w_down, mix_norm, w_in,
          na_q_norm, na_k_norm, na_rpb, sg_v_norm, sg_w, sg_b, gqa_q_norm, gqa_k_norm,
          w_out, ffn2_norm, ffn2_w_gate, ffn2_w_up, ffn2_w_down):
    b, s, _ = x.shape
    cos, sin = axial_rope_tables(s)
    for l in range(DEPTH):
        x = x + 0.5 * swiglu_ffn(rms_norm(x, ffn1_norm[l]), ffn1_w_gate[l], ffn1_w_up[l], ffn1_w_down[l])
        h = rms_norm(x, mix_norm[l])
        z = h @ w_in[l]
        qa, ka, va, ub, vb, qc, kc, vc = jnp.split(z, IN_SPLITS, axis=-1)
        heads = lambda t, n: t.reshape(b, s, n, HEAD_DIM)
        qa = rms_norm(heads(qa, NA_HEADS), na_q_norm[l])
        ka = rms_norm(heads(ka, NA_HEADS), na_k_norm[l])
        out_a = neighbourhood_attention(qa, ka, heads(va, NA_HEADS), na_rpb[l])
        out_b = spatial_gating(ub, vb, sg_v_norm[l], sg_w[l], sg_b[l])
        qc = apply_rope(rms_norm(heads(qc, GQA_Q_HEADS), gqa_q_norm[l]), cos, sin)
        kc = apply_rope(rms_norm(heads(kc, GQA_KV_HEADS), gqa_k_norm[l]), cos, sin)
        out_c = gqa_attention(qc, kc, heads(vc, GQA_KV_HEADS))
        x = x + jnp.concatenate([out_a, out_b, out_c], axis=-1) @ w_out[l]
        x = x + 0.5 * swiglu_ffn(rms_norm(x, ffn2_norm[l]), ffn2_w_gate[l], ffn2_w_up[l], ffn2_w_down[l])
    return x


def setup_inputs(seed: int = 0) -> dict:
    key = jax.random.key(seed)
    ks = jax.random.split(key, 24)
    f32 = jnp.float32
    nrm = lambda k, shape, sc: jax.random.normal(k, shape, f32) * sc
    gain = lambda k, shape: 1.0 + 0.01 * jax.random.normal(k, shape, f32)
    L = DEPTH
    return {
        "x_prompt": jax.random.normal(ks[0], (BATCH, SEQ, D_MODEL), f32),
        "x_sample": jax.random.normal(ks[1], (DEC_BATCH, DEC_SEQ, D_MODEL), f32),
        "ffn1_norm": gain(ks[2], (L, D_MODEL)),
        "ffn1_w_gate": nrm(ks[3], (L, D_MODEL, D_FF), D_MODEL ** -0.5),
        "ffn1_w_up": nrm(ks[4], (L, D_MODEL, D_FF), D_MODEL ** -0.5),
        "ffn1_w_down": nrm(ks[5], (L, D_FF, D_MODEL), D_FF ** -0.5),
        "mix_norm": gain(ks[6], (L, D_MODEL)),
        "w_in": nrm(ks[7], (L, D_MODEL, D_IN), D_MODEL ** -0.5),
        "na_q_norm": gain(ks[8], (L, HEAD_DIM)),
        "na_k_norm": gain(ks[9], (L, HEAD_DIM)),
        "na_rpb": nrm(ks[10], (L, NA_HEADS, 2 * WIN_R - 1, 2 * WIN_C - 1), 0.1),
        "sg_v_norm": gain(ks[11], (L, SG_GROUPS, SG_GROUP_DIM)),
        "sg_w": nrm(ks[12], (L, SG_GROUPS, SG_CHUNK, SG_CHUNK), SG_CHUNK ** -0.5),
        "sg_b": 1.0 + nrm(ks[13], (L, SG_GROUPS, SG_CHUNK), 0.02),
        "gqa_q_norm": gain(ks[14], (L, HEAD_DIM)),
        "gqa_k_norm": gain(ks[15], (L, HEAD_DIM)),
        "w_out": nrm(ks[16], (L, D_MIX, D_MODEL), D_MIX ** -0.5),
        "ffn2_norm": gain(ks[17], (L, D_MODEL)),
        "ffn2_w_gate": nrm(ks[18], (L, D_MODEL, D_FF), D_MODEL ** -0.5),
        "ffn2_w_up": nrm(ks[19], (L, D_MODEL, D_FF), D_MODEL ** -0.5),
        "ffn2_w_down": nrm(ks[20], (L, D_FF, D_MODEL), D_FF ** -0.5),
    }


def reference(x_prompt, x_sample, ffn1_norm, ffn1_w_gate, ffn1_w_up, ffn1_w_down, mix_norm, w_in,
              na_q_norm, na_k_norm, na_rpb, sg_v_norm, sg_w, sg_b, gqa_q_norm, gqa_k_norm,
              w_out, ffn2_norm, ffn2_w_gate, ffn2_w_up, ffn2_w_down):
    y_prompt = trunk(x_prompt, ffn1_norm, ffn1_w_gate, ffn1_w_up, ffn1_w_down, mix_norm, w_in,
                     na_q_norm, na_k_norm, na_rpb, sg_v_norm, sg_w, sg_b, gqa_q_norm, gqa_k_norm,
                     w_out, ffn2_norm, ffn2_w_gate, ffn2_w_up, ffn2_w_down)
    y_sample = trunk(x_sample, ffn1_norm, ffn1_w_gate, ffn1_w_up, ffn1_w_down, mix_norm, w_in,
                     na_q_norm, na_k_norm, na_rpb, sg_v_norm, sg_w, sg_b, gqa_q_norm, gqa_k_norm,
                     w_out, ffn2_norm, ffn2_w_gate, ffn2_w_up, ffn2_w_down)
    return (y_prompt, y_sample)
```

```python
import os
import numpy as np
from contextlib import ExitStack
import concourse.bass as bass
import concourse.mybir as mybir
from concourse.bass_utils import run_bass_kernel_spmd

F32 = mybir.dt.float32
BF16 = mybir.dt.bfloat16
AF = mybir.ActivationFunctionType
ALU = mybir.AluOpType
AX = mybir.AxisListType

SEQ = 8192
D = 1024
FF = 2816
NFF = 22
T = 512
NT = SEQ // T
DEPTH = 4
EPS = 1e-6
NCORES = 6


class Buf:
    __slots__ = ("name", "w", "r", "nowaw", "excl")

    def __init__(self, name, nowaw=False, excl=False):
        self.name = name
        self.w = {}
        self.r = {}
        self.nowaw = nowaw
        self.excl = excl


class Sched:
    ENGS = ("pe", "act", "dve", "pool", "sp")

    def __init__(self, nc):
        self.nc = nc
        self.streams = {e: [] for e in self.ENGS}
        self.cnt = {e: 0 for e in self.ENGS}
        self.waited = {e: {} for e in self.ENGS}
        self.sems = {}
        self.dcum = {}
        self._ctx = []
        self.ninstr = 0

    def sem(self, name):
        if name not in self.sems:
            cm = self.nc.semaphore(name)
            self.sems[name] = cm.__enter__()
            self._ctx.append(cm)
        return name

    def _deps(self, eng, reads, writes, own):
        deps = {}

        def add(d, skip_own):
            for s, v in d.items():
                if skip_own and s == own:
                    continue
                if deps.get(s, 0) < v:
                    deps[s] = v
        for b in reads:
            add(b.w, False)
            if b.excl:
                add(b.r, True)
        skip = (eng == "pe")
        for b in writes:
            if not b.nowaw:
                add(b.w, skip)
            add(b.r, skip)
        out = []
        wt = self.waited[eng]
        for s, v in deps.items():
            if wt.get(s, 0) >= v:
                continue
            wt[s] = v
            out.append((s, v))
        return out

    @staticmethod
    def _commit(tok, reads, writes):
        s, v = tok
        for b in reads:
            if b.r.get(s, 0) < v:
                b.r[s] = v
        for b in writes:
            if b.w.get(s, 0) < v:
                b.w[s] = v

    def op(self, eng, emit, reads=(), writes=()):
        own = self.sem("e_" + eng)
        waits = self._deps(eng, reads, writes, own)
        self.cnt[eng] += 1
        tok = (own, self.cnt[eng])
        self.streams[eng].append((waits, emit, own, 1))
        self._commit(tok, reads, writes)
        return tok

    def dma(self, eng, semname, emit, n, reads=(), writes=()):
        s = self.sem(semname)
        prev = self.dcum.get(s, 0)
        waits = self._deps(eng, reads, writes, None)
        wt = self.waited[eng]
        if prev > 0 and wt.get(s, 0) < prev:
            wt[s] = prev
            waits.append((s, prev))
        cum = prev + 16 * n
        self.dcum[s] = cum
        tok = (s, cum)
        self.streams[eng].append((waits, emit, s, 16))
        self._commit(tok, reads, writes)
        return tok

    def final_wait(self, eng, bufs):
        deps = {}
        for b in bufs:
            for s, v in b.w.items():
                deps[s] = max(deps.get(s, 0), v)
        self.streams[eng].append((list(deps.items()), None, None, 0))

    def emit_all(self):
        nc = self.nc
        sems = self.sems
        streams = self.streams

        def run(engname, eng):
            for waits, emit, s, inc in streams[engname]:
                for ws, wv in waits:
                    eng.wait_ge(sems[ws], wv)
                if emit is None:
                    continue
                r = emit(eng)
                if isinstance(r, (list, tuple)):
                    for ins in r:
                        ins.then_inc(sems[s], inc)
                else:
                    r.then_inc(sems[s], inc)

        with nc.Block() as block:
            @block.tensor
            def _(e):
                run("pe", e)

            @block.scalar
            def _(e):
                run("act", e)

            @block.vector
            def _(e):
                run("dve", e)

            @block.gpsimd
            def _(e):
                run("pool", e)

            @block.sync
            def _(e):
                run("sp", e)

    def close(self):
        for cm in reversed(self._ctx):
            cm.__exit__(None, None, None)


NA_CLASS_QB = (2, 0, 1, 62, 63)


def na_blocks(qb):
    if qb == 0:
        return 1, [0, 1, 2, 3]
    if qb == 1:
        return 2, [0, 1, 2, 3]
    if qb == 62:
        return 3, [60, 61, 62, 63]
    if qb == 63:
        return 4, [60, 61, 62, 63]
    return 0, [qb - 2, qb - 1, qb, qb + 1, qb + 2]


def na_index_table():
    idx = np.full((5, 128, 5, 128), 465, np.int64)
    key = np.arange(128)
    qry = np.arange(128)
    for cls, qb in enumerate(NA_CLASS_QB):
        _, kbs = na_blocks(qb)
        qr = 2 * qb + qry // 64
        qc = qry % 64
        rs = np.clip(qr - 4, 0, 120)
        cs = np.clip(qc - 8, 0, 48)
        for j, kb in enumerate(kbs):
            kr = 2 * kb + key // 64
            kc = key % 64
            inwin = ((kr[:, None] >= rs[None, :]) & (kr[:, None] < rs[None, :] + 8) &
                     (kc[:, None] >= cs[None, :]) & (kc[:, None] < cs[None, :] + 16))
            flat = (kr[:, None] - qr[None, :] + 7) * 31 + (kc[:, None] - qc[None, :] + 15)
            idx[cls, :, j, :] = np.where(inwin, flat, 465)
    return idx


def rope_tables():
    t = np.arange(SEQ)
    row = (t // 64).astype(np.float32)
    col = (t % 64).astype(np.float32)
    inv = (np.float32(10000.0) ** (-np.arange(16, dtype=np.float32) / np.float32(16))).astype(np.float32)
    ang = np.concatenate([row[:, None] * inv, col[:, None] * inv], axis=-1).astype(np.float32)
    c = np.cos(ang).astype(np.float32).T
    s = np.sin(ang).astype(np.float32).T
    C = np.concatenate([c, c, c, c], axis=0)
    Ssig = np.concatenate([-s, s, -s, s], axis=0)
    return np.ascontiguousarray(C), np.ascontiguousarray(Ssig)


def const_mats():
    cm = np.zeros((128, 5, 128), np.float32)
    cm[:, 0, :] = np.eye(128)
    cm[:, 1, :] = 1.0 / 1024.0
    for b in range(2):
        cm[b * 64:(b + 1) * 64, 2, b * 64:(b + 1) * 64] = 1.0 / 64.0
    m = np.arange(128)
    perm = (m // 64) * 64 + ((m % 64) + 32) % 64
    cm[perm, 3, m] = 1.0
    cm[:, 4, :] = 8.0 * np.eye(128)
    return cm


def prep_shared(inp):
    L = DEPTH
    h64 = lambda a, i: np.arange(a + 64 * i, a + 64 * (i + 1))
    cols = [np.arange(0, 256), np.arange(256, 512), np.arange(768, 1024)]
    for c in range(4):
        cols += [h64(1280, c), h64(1280, 4 + c)]
    cols += [np.arange(1792, 1920), np.arange(512, 768), np.arange(1920, 2048), np.arange(1024, 1280)]
    cols = np.concatenate(cols)
    assert cols.size == 2048 and np.unique(cols).size == 2048
    rows = [np.arange(0, 512)]
    for c in range(4):
        rows += [h64(512, c), h64(512, 4 + c)]
    rows = np.concatenate(rows)
    f = lambda k: np.asarray(inp[k], np.float32)
    sh = {}
    sh["wg1"] = np.ascontiguousarray(f("ffn1_w_gate"))
    sh["wu1"] = np.ascontiguousarray(f("ffn1_w_up"))
    sh["wd1"] = np.ascontiguousarray(f("ffn1_w_down"))
    sh["wg2"] = np.ascontiguousarray(f("ffn2_w_gate"))
    sh["wu2"] = np.ascontiguousarray(f("ffn2_w_up"))
    sh["wd2"] = np.ascontiguousarray(f("ffn2_w_down"))
    sh["win"] = np.ascontiguousarray(f("w_in")[:, :, cols])
    sh["wo"] = np.ascontiguousarray(f("w_out")[:, rows, :])
    g = np.zeros((L, 128, 28), np.float32)
    for i, k in enumerate(("ffn1_norm", "mix_norm", "ffn2_norm")):
        g[:, :, 8 * i:8 * i + 8] = f(k).reshape(L, 8, 128).transpose(0, 2, 1)
    for i, k in enumerate(("na_q_norm", "na_k_norm", "gqa_q_norm", "gqa_k_norm")):
        g[:, :, 24 + i] = np.tile(f(k), (1, 2))
    sh["gains"] = g
    sh["vnb"] = np.ascontiguousarray(np.broadcast_to(f("sg_v_norm").reshape(L, 1, 256), (L, 128, 256)))
    sb = f("sg_b")
    sgb = np.zeros((L, 128, 2, 4, 128), np.float32)
    for o in range(2):
        sgb[:, 0:64, o] = sb[:, 2 * o][:, None, None, :]
        sgb[:, 64:128, o] = sb[:, 2 * o + 1][:, None, None, :]
    sh["sgb"] = sgb.reshape(L, 128, 2, 512)
    sh["sgw"] = np.ascontiguousarray(f("sg_w").transpose(0, 3, 1, 2))
    rpb = f("na_rpb").reshape(L, 4, 465)
    rpb_ext = np.concatenate([rpb, np.full((L, 4, 1), -30000.0, np.float32)], axis=-1)
    idx = na_index_table()
    nab = rpb_ext[:, :, idx]
    sh["nab"] = np.ascontiguousarray(nab.transpose(0, 2, 3, 1, 4, 5)).reshape(L, 5, 128, 2560)
    C, Ssig = rope_tables()
    sh["ropeC"] = C
    sh["ropeS"] = Ssig
    sh["consts"] = const_mats()
    return sh


def build(nlayers=DEPTH, debug=False, stop=None, ntiles=NT, ntiles_b=None):
    nc = bass.Bass("TRN2", target_bir_lowering=False)
    L = nlayers
    dt_in = lambda name, shape: nc.dram_tensor(name, shape, F32, kind="ExternalInput").ap()
    skind = "ExternalOutput" if debug else "Internal"
    scr = lambda name, shape, dt: nc.dram_tensor(name, shape, dt, kind=skind).ap()

    x_in = dt_in("x", [SEQ, D])
    W32 = {k: dt_in(k, [DEPTH, D, FF]) for k in ("wg1", "wu1", "wg2", "wu2")}
    W32["wd1"] = dt_in("wd1", [DEPTH, FF, D])
    W32["wd2"] = dt_in("wd2", [DEPTH, FF, D])
    W32["win"] = dt_in("win", [DEPTH, D, 2048])
    W32["wo"] = dt_in("wo", [DEPTH, D, D])
    gains_d = dt_in("gains", [DEPTH, 128, 28])
    vnb_d = dt_in("vnb", [DEPTH, 128, 256])
    sgb_d = dt_in("sgb", [DEPTH, 128, 2, 512])
    sgw_d = dt_in("sgw", [DEPTH, 128, 4, 128])
    nab_d = dt_in("nab", [DEPTH, 5, 128, 2560])
    ropeC_d = dt_in("ropeC", [128, SEQ])
    ropeS_d = dt_in("ropeS", [128, SEQ])
    consts_d = dt_in("consts", [128, 5, 128])
    y_out = nc.dram_tensor("y", [SEQ, D], F32, kind="ExternalOutput").ap()

    X1 = scr("X1", [128, 8, SEQ], F32)
    X0 = scr("X0", [128, 8, SEQ], F32)
    QO = scr("QO", [128, 8, SEQ], BF16)
    KA = scr("KA", [128, 2, SEQ], BF16)
    KC = scr("KC", [128, SEQ], BF16)
    VA = scr("VA", [128, 64, 384], BF16)
    VC = scr("VC", [128, 64, 192], BF16)
    WB = {}
    for l in range(L):
        for k in ("wg1", "wu1", "wg2", "wu2"):
            WB[k, l] = nc.dram_tensor(f"{k}b{l}", [D, FF], BF16).ap()
        for k in ("wd1", "wd2"):
            WB[k, l] = nc.dram_tensor(f"{k}b{l}", [FF, D], BF16).ap()
        WB["win", l] = nc.dram_tensor(f"winb{l}", [D, 2048], BF16).ap()
        WB["wo", l] = nc.dram_tensor(f"wob{l}", [D, D], BF16).ap()
        WB["nab", l] = nc.dram_tensor(f"nabb{l}", [5, 128, 2560], BF16).ap()

    S = Sched(nc)
    B = {}

    def buf(name, nowaw=False, excl=False):
        B[name] = Buf(name, nowaw, excl)
        return B[name]

    for n in ("X0", "X1", "QO", "KA", "KC", "VA", "VC", "y"):
        buf("d_" + n, nowaw=True)
    for key in WB:
        buf(f"d_{key[0]}{key[1]}", nowaw=True)
    DB = lambda k, l: B[f"d_{k}{l}"]

    with ExitStack() as es:
        def sb(name, shape, dt):
            t = es.enter_context(nc.sbuf_tensor(name, shape, dt))
            buf(name)
            return t

        psum = es.enter_context(nc.psum_tensor("psum", [128, 8, 512], F32))
        PB = [buf(f"ps{i}", excl=True) for i in range(8)]

        cst32 = sb("cst32", [128, 5, 128], F32)
        cstb = sb("cstb", [128, 5, 128], BF16)
        epsb = sb("epsb", [128, 1], F32)
        gsb = [sb(f"gsb{i}", [128, 28], F32) for i in range(2)]
        xs = [sb(f"xs{i}", [128, 8, T], F32) for i in range(2)]
        hb = sb("hb", [128, 8, T], BF16)
        ab = sb("ab", [128, NFF, T], BF16)
        NW = 6
        wsl = [sb(f"wsl{i}", [128, 2048], BF16) for i in range(NW)]
        qo = [sb(f"qo{i}", [128, 8, T], BF16) for i in range(2)]
        sq = [sb(f"sq{i}", [128, T], BF16) for i in range(2)]
        rstd = [sb(f"rstd{i}", [128, T], F32) for i in range(2)]
        tmpa = [sb(f"tmpa{i}", [128, T], F32) for i in range(2)]
        OVL = 80 * 1024 // 2
        ovl = es.enter_context(nc.sbuf_tensor("ovl", [128, OVL], BF16))
        cur = {"A": 0, "B": 0}
        pool_bufs = {"A": [], "B": []}

        def carve(ph, name, shape, dt):
            n = 1
            for d_ in shape[1:]:
                n *= d_
            nel = n * (2 if dt == F32 else 1)
            nel = (nel + 15) // 16 * 16
            a0 = cur[ph]
            cur[ph] += nel
            assert cur[ph] <= OVL, (ph, name, cur[ph])
            ap = ovl[:, a0:a0 + n * (2 if dt == F32 else 1)]
            if dt == F32:
                ap = ap.bitcast(F32)
            if len(shape) == 3:
                ap = ap.rearrange("p (a b) -> p a b", a=shape[1])
            pool_bufs[ph].append(buf(name))
            return ap

        vnbt = [carve("A", f"vnbt{i}", [128, 256], F32) for i in range(2)]
        sgbt = [carve("A", f"sgbt{i}", [128, 2, 512], F32) for i in range(1)] * 2
        sgw32 = carve("A", "sgw32", [128, 4, 128], F32)
        sgwb = [carve("A", f"sgwb{i}", [128, 4, 128], BF16) for i in range(1)] * 2
        tmpb = [carve("A", f"tmpb{i}", [128, T], F32) for i in range(1)]
        tmpc = [carve("A", f"tmpc{i}", [128, T], F32) for i in range(2)]
        qnb = [carve("A", f"qnb{i}", [128, T], BF16) for i in range(2)]
        ropeCt = [carve("A", f"ropeC{i}", [128, T], F32) for i in range(2)]
        ropeSt = [carve("A", f"ropeS{i}", [128, T], F32) for i in range(2)]
        ug = carve("A", "ug", [128, 2, T], F32)
        gv = [carve("A", f"gv{i}", [128, 256], F32) for i in range(2)]
        gv2 = [carve("A", f"gv2{i}", [128, 256], F32) for i in range(2)]
        ss4 = [carve("A", f"ss4{i}", [128, 16], F32) for i in range(2)]
        vn = carve("A", "vn", [128, 4, 256], BF16)
        kast = [carve("A", f"kast{i}", [128, 2, T], BF16) for i in range(2)]
        kcst = [carve("A", f"kcst{i}", [128, T], BF16) for i in range(2)]
        vast = [carve("A", f"vast{i}", [128, 4, 384], BF16) for i in range(2)]
        vcst = [carve("A", f"vcst{i}", [128, 4, 192], BF16) for i in range(2)]
        kcr = carve("B", "kcr", [128, SEQ], BF16)
        vcr = carve("B", "vcr", [128, 64, 192], BF16)
        kaw = [carve("B", f"kaw{i}", [128, 2, 1024], BF16) for i in range(1)] * 2
        vaw = [carve("B", f"vaw{i}", [128, 8, 384], BF16) for i in range(1)] * 2
        nabr = [carve("B", f"nabr{i}", [128, 2560], BF16) for i in range(1)] * 2
        nabe = carve("B", "nabe", [128, 2560], BF16)
        pt = [carve("B", f"pt{i}", [128, 2, T], BF16) for i in range(3)]
        ptn = [carve("B", f"ptn{i}", [128, 5, 128], BF16) for i in range(2)]
        ao = carve("B", "ao", [128, 6, T], BF16)
        rc = [carve("B", f"rc{i}", [128, T], F32) for i in range(2)]

        def phase_switch(old, new):
            allw = {}
            for b_ in pool_bufs[old]:
                for d_ in (b_.w, b_.r):
                    for s_, v_ in d_.items():
                        if allw.get(s_, 0) < v_:
                            allw[s_] = v_
            for b_ in pool_bufs[new]:
                for s_, v_ in allw.items():
                    if b_.w.get(s_, 0) < v_:
                        b_.w[s_] = v_
                    if b_.r.get(s_, 0) < v_:
                        b_.r[s_] = v_

        ctr = {"w": 0}

        def mm(calls, reads, writes):
            def emit(e, calls=calls):
                r = None
                for (o, l_, rh, st, sp, tp) in calls:
                    if tp is None:
                        r = e.matmul(o, lhsT=l_, rhs=rh, start=st, stop=sp)
                    else:
                        r = e.matmul(o, lhsT=l_, rhs=rh, start=st, stop=sp, tile_position=tp)
                return r
            S.op("pe", emit, reads=reads, writes=writes)

        def act(out, in_, func, reads, writes, **kw):
            S.op("act", lambda e: e.activation(out, in_, func, **kw), reads=reads, writes=writes)

        def dve(fn, reads, writes, eng="dve"):
            S.op(eng, fn, reads=reads, writes=writes)

        def wload(key, l, src3, k, c):
            assert (key, l) in cast_ready, (key, l)
            i = ctr["w"] % NW
            ctr["w"] += 1
            view = wsl[i][:, 0:k * c].rearrange("p (k c) -> p k c", k=k)
            S.dma("sp", f"w{i}", lambda e: [e.dma_start(out=view, in_=src3)], 1,
                  reads=[DB(key, l)], writes=[B[f"wsl{i}"]])
            return view, B[f"wsl{i}"]

        def wsrc(key, l, r0, nk, c0, c):
            return WB[key, l].rearrange("(k p) c -> p k c", p=128)[:, r0:r0 + nk, c0:c0 + c]

        pending = []
        cast_ready = set()
        ctr["cast"] = 0

        def queue_casts(l, keys):
            def one(key, l, dst, src):
                def go():
                    i = ctr["cast"] % 4
                    ctr["cast"] += 1
                    S.dma("pool", f"cast{i}", lambda e: [e.dma_start(out=dst, in_=src)], 1, reads=[], writes=[DB(key, l)])
                return go
            for key in keys:
                if key == "nab":
                    pending.append((None, one("nab", l, WB["nab", l].rearrange("c p f -> (c p) f"), nab_d[l].rearrange("c p f -> (c p) f"))))
                else:
                    nrows = FF if key in ("wd1", "wd2") else D
                    for r0 in range(0, nrows, 256):
                        r1 = min(nrows, r0 + 256)
                        pending.append((None, one(key, l, WB[key, l][r0:r1, :], W32[key][l][r0:r1, :])))
                pending.append(((key, l), None))

        def pump_casts(n):
            while pending and n > 0:
                mark, go = pending.pop(0)
                if go is None:
                    cast_ready.add(mark)
                else:
                    go()
                    n -= 1
            while pending and pending[0][1] is None:
                cast_ready.add(pending.pop(0)[0])

        KA_ = ("wg1", "wu1", "wd1", "win")
        KB_ = ("nab", "wo", "wg2", "wu2", "wd2")

        def layer_params(l):
            i = l % 2
            S.dma("sp", f"lp{i}", lambda e: [e.dma_start(out=gsb[i][:], in_=gains_d[l]),
                                             e.dma_start(out=vnbt[i][:], in_=vnb_d[l]),
                                             e.dma_start(out=sgbt[i][:], in_=sgb_d[l])], 3,
                  writes=[B[f"gsb{i}"], B[f"vnbt{i}"], B["sgbt0"]])
            S.dma("sp", "lpw", lambda e: [e.dma_start(out=sgw32[:], in_=sgw_d[l])], 1, writes=[B["sgw32"]])
            dve(lambda e: e.tensor_copy(sgwb[i][:], sgw32[:]), [B["sgw32"]], [B["sgwb0"]])


        def rmsnorm_x(xt, xbuf, gcol0, gi):
            calls = []
            for c in range(8):
                j = c % 2
                act(sq[j][:], xt[:, c, :], AF.Square, [xbuf], [B[f"sq{j}"]])
                mm([(psum[:, 6, :], cstb[:, 1, :], sq[j][:], c == 0, c == 7, None)], [B[f"sq{j}"], B["cstb"]], [PB[6]])
            act(rstd[0][:], psum[:, 6, :], AF.Sqrt, [PB[6], B["epsb"]], [B["rstd0"]], bias=epsb[:, 0:1], scale=1.0)
            dve(lambda e: e.reciprocal(rstd[0][:], rstd[0][:]), [B["rstd0"]], [B["rstd0"]])
            for c in range(8):
                dve(lambda e, c=c: e.scalar_tensor_tensor(hb[:, c, :], xt[:, c, :], gsb[gi][:, gcol0 + c:gcol0 + c + 1],
                                                           rstd[0][:], ALU.mult, ALU.mult),
                    [xbuf, B["rstd0"], B[f"gsb{gi}"]], [B["hb"]])

        def ffn_gen(l, kg, ku, kd, xt, xbuf, gu_banks=((0, 1), (2, 3)), d_banks=(4, 5), tanh_silu=False):
            groups = [(2 * i, 2) for i in range(11)]
            for (m0, nm) in groups:
                wg, wgb = wload(kg, l, wsrc(kg, l, 0, 8, m0 * 128, nm * 128), 8, nm * 128)
                wu, wub = wload(ku, l, wsrc(ku, l, 0, 8, m0 * 128, nm * 128), 8, nm * 128)
                for ml in range(nm):
                    m = m0 + ml
                    bg, bu = gu_banks[m % len(gu_banks)]
                    mm([(psum[:, bg, :], wg[:, k, ml * 128:(ml + 1) * 128], hb[:, k, :], k == 0, k == 7, None) for k in range(8)],
                       [wgb, B["hb"]], [PB[bg]])
                    mm([(psum[:, bu, :], wu[:, k, ml * 128:(ml + 1) * 128], hb[:, k, :], k == 0, k == 7, None) for k in range(8)],
                       [wub, B["hb"]], [PB[bu]])
                    j = m % 2
                    if tanh_silu:
                        act(tmpa[j][:], psum[:, bg, :], AF.Tanh, [PB[bg]], [B[f"tmpa{j}"]], scale=0.5)
                        dve(lambda e, j=j, bg=bg: e.scalar_tensor_tensor(tmpa[j][:], tmpa[j][:], 1.0, psum[:, bg, :], ALU.add, ALU.mult),
                            [B[f"tmpa{j}"], PB[bg]], [B[f"tmpa{j}"]])
                    else:
                        act(tmpa[j][:], psum[:, bg, :], AF.Silu, [PB[bg]], [B[f"tmpa{j}"]])
                    dve(lambda e, m=m, j=j, bu=bu: e.tensor_tensor(ab[:, m, :], tmpa[j][:], psum[:, bu, :], ALU.mult),
                        [B[f"tmpa{j}"], PB[bu]], [B["ab"]])
                    yield
            rs = 0.25 if tanh_silu else 0.5
            for npair in range(4):
                for (mr0, nmr) in ((0, 8), (8, 8), (16, 6)):
                    src = WB[kd, l].rearrange("(m p) c -> p m c", p=128)[:, mr0:mr0 + nmr, npair * 256:npair * 256 + 256]
                    wd, wdb = wload(kd, l, src, nmr, 256)
                    for nn in range(2):
                        mm([(psum[:, d_banks[nn], :], wd[:, mi, nn * 128:(nn + 1) * 128], ab[:, mr0 + mi, :],
                             mr0 + mi == 0, mr0 + mi == NFF - 1, None) for mi in range(nmr)],
                           [wdb, B["ab"]], [PB[d_banks[nn]]])
                    if mr0 < 16:
                        yield
                for nn in range(2):
                    n = npair * 2 + nn
                    dve(lambda e, n=n, nn=nn: e.scalar_tensor_tensor(xt[:, n, :], psum[:, d_banks[nn], :], rs, xt[:, n, :], ALU.mult, ALU.add),
                        [PB[d_banks[nn]], xbuf], [xbuf])
                yield

        def ffn(l, kg, ku, kd, xt, xbuf):
            for _ in ffn_gen(l, kg, ku, kd, xt, xbuf):
                pass

        def load_x(t, slot, src, srcbuf):
            S.dma("sp", f"xs{slot}", lambda e: [e.dma_start(out=xs[slot][:], in_=src[:, :, t * T:(t + 1) * T])], 1,
                  reads=[srcbuf], writes=[B[f"xs{slot}"]])

        def load_x_tokmajor(t, slot):
            for b in range(4):
                i = ctr["w"] % NW
                ctr["w"] += 1
                tmv = wsl[i][:, 0:2048].bitcast(F32)
                r0 = t * T + b * 128
                S.dma("sp", f"w{i}", lambda e, tmv=tmv, r0=r0: [e.dma_start(out=tmv, in_=x_in[r0:r0 + 128, :])], 1,
                      writes=[B[f"wsl{i}"]])
                for hf in range(2):
                    def emit(e, tmv=tmv, hf=hf):
                        r = None
                        for c4 in range(4):
                            c = hf * 4 + c4
                            r = e.transpose(psum[:, hf, c4 * 128:(c4 + 1) * 128], tmv[:, c * 128:(c + 1) * 128], cst32[:, 0, :])
                        return r
                    S.op("pe", emit, reads=[B[f"wsl{i}"], B["cst32"]], writes=[PB[hf]])
                    dve(lambda e, hf=hf, b=b: e.tensor_copy(xs[slot][:, hf * 4:hf * 4 + 4, b * 128:(b + 1) * 128],
                                                            psum[:, hf, :].rearrange("p (c t) -> p c t", c=4)),
                        [PB[hf]], [B[f"xs{slot}"]])

        def store_x(t, slot, dst, dstbuf):
            S.dma("pool", f"sx{slot}", lambda e: [e.dma_start(out=dst[:, :, t * T:(t + 1) * T], in_=xs[slot][:])], 1,
                  reads=[B[f"xs{slot}"]], writes=[dstbuf])

        def store_x_tokmajor(t, slot):
            for b in range(4):
                i = ctr["w"] % NW
                ctr["w"] += 1
                tmv = wsl[i][:, 0:2048].bitcast(F32)
                for hf in range(2):
                    def emit(e, hf=hf, b=b):
                        r = None
                        for c4 in range(4):
                            c = hf * 4 + c4
                            r = e.transpose(psum[:, hf, c4 * 128:(c4 + 1) * 128], xs[slot][:, c, b * 128:(b + 1) * 128], cst32[:, 0, :])
                        return r
                    S.op("pe", emit, reads=[B[f"xs{slot}"], B["cst32"]], writes=[PB[hf]])
                    dve(lambda e, hf=hf, tmv=tmv: e.tensor_copy(tmv[:, hf * 512:(hf + 1) * 512], psum[:, hf, :]),
                        [PB[hf]], [B[f"wsl{i}"]])
                r0 = t * T + b * 128
                S.dma("pool", f"sy{i}", lambda e, tmv=tmv, r0=r0: [e.dma_start(out=y_out[r0:r0 + 128, :], in_=tmv)], 1,
                      reads=[B[f"wsl{i}"]], writes=[B["d_y"]])

        def qk_pre(l, m, zb, gcol, rope):
            gi = l % 2
            j = m % 2
            act(sq[j][:], psum[:, zb, :], AF.Square, [PB[zb]], [B[f"sq{j}"]])
            if rope:
                act(qnb[j][:], psum[:, zb, :], AF.Copy, [PB[zb], B[f"gsb{gi}"]], [B[f"qnb{j}"]], scale=gsb[gi][:, gcol:gcol + 1])

        def qk_chunk(l, m, zb, gcol, dst, dstbuf, rope, tslot):
            gi = l % 2
            j = m % 2
            mm([(psum[:, 6, :], cstb[:, 2, :], sq[j][:], True, True, None)], [B[f"sq{j}"], B["cstb"]], [PB[6]])
            if rope:
                mm([(psum[:, 7, :], cstb[:, 3, :], qnb[j][:], True, True, None)], [B[f"qnb{j}"], B["cstb"]], [PB[7]])
            act(rstd[1][:], psum[:, 6, :], AF.Sqrt, [PB[6], B["epsb"]], [B["rstd1"]], bias=epsb[:, 0:1], scale=1.0)
            dve(lambda e: e.reciprocal(rstd[1][:], rstd[1][:]), [B["rstd1"]], [B["rstd1"]])
            if not rope:
                dve(lambda e: e.scalar_tensor_tensor(dst, psum[:, zb, :], gsb[gi][:, gcol:gcol + 1], rstd[1][:], ALU.mult, ALU.mult),
                    [PB[zb], B["rstd1"], B[f"gsb{gi}"]], [dstbuf])
                return
            dve(lambda e: e.scalar_tensor_tensor(tmpb[0][:], psum[:, zb, :], gsb[gi][:, gcol:gcol + 1], ropeCt[tslot][:], ALU.mult, ALU.mult),
                [PB[zb], B[f"gsb{gi}"], B[f"ropeC{tslot}"]], [B["tmpb0"]])
            dve(lambda e: e.tensor_tensor(tmpc[0][:], psum[:, 7, :], ropeSt[tslot][:], ALU.mult), [PB[7], B[f"ropeS{tslot}"]], [B["tmpc0"]])
            dve(lambda e: e.tensor_tensor(tmpb[0][:], tmpb[0][:], tmpc[0][:], ALU.add), [B["tmpb0"], B["tmpc0"]], [B["tmpb0"]])
            dve(lambda e: e.tensor_tensor(dst, tmpb[0][:], rstd[1][:], ALU.mult), [B["tmpb0"], B["rstd1"]], [dstbuf])

        def pass_a_tile(l, t, first):
            gi = l % 2
            slot = t % 2
            xt, xbuf = xs[slot], B[f"xs{slot}"]
            if first:
                load_x_tokmajor(t, slot)
            else:
                load_x(t, slot, X0, B["d_X0"])
            S.dma("sp", f"rope{slot}", lambda e: [e.dma_start(out=ropeCt[slot][:], in_=ropeC_d[:, t * T:(t + 1) * T]),
                                                  e.dma_start(out=ropeSt[slot][:], in_=ropeS_d[:, t * T:(t + 1) * T])], 2,
                  writes=[B[f"ropeC{slot}"], B[f"ropeS{slot}"]])
            rmsnorm_x(xt, xbuf, 0, gi)
            ffn(l, "wg1", "wu1", "wd1", xt, xbuf)
            rmsnorm_x(xt, xbuf, 8, gi)
            store_x(t, slot, X1, B["d_X1"])
            if stop == "ffn1":
                return
            q, qbuf = qo[slot], B[f"qo{slot}"]
            wt1a, wt1ab = wload("win", l, wsrc("win", l, 0, 4, 1408, 384), 4, 384)
            wt1c, wt1cb = wload("win", l, wsrc("win", l, 4, 4, 1408, 384), 4, 384)
            wt2, wt2b = wload("win", l, wsrc("win", l, 0, 8, 1792, 256), 8, 256)
            for s in range(4):
                mm([(psum[:, 4, 0:384], hb[:, k, s * 128:(s + 1) * 128], (wt1a if k < 4 else wt1c)[:, k % 4, :], k == 0, k == 7, None) for k in range(8)],
                   [wt1ab, wt1cb, B["hb"]], [PB[4]])
                mm([(psum[:, 5, 0:256], hb[:, k, s * 128:(s + 1) * 128], wt2[:, k, :], k == 0, k == 7, None) for k in range(8)],
                   [wt2b, B["hb"]], [PB[5]])
                va_dst = vast[slot]
                def cpva(e, s=s, va_dst=va_dst):
                    e.tensor_copy(va_dst[:, s, 0:64], psum[:, 4, 0:64])
                    e.tensor_copy(va_dst[:, s, 128:256], psum[:, 4, 64:192])
                    e.tensor_copy(va_dst[:, s, 320:384], psum[:, 4, 192:256])
                    e.tensor_copy(vcst[slot][:, s, 0:64], psum[:, 4, 256:320])
                    return e.tensor_copy(vcst[slot][:, s, 128:192], psum[:, 4, 320:384])
                dve(cpva, [PB[4]], [B[f"vast{slot}"], B[f"vcst{slot}"]])
                j = s % 2
                act(gv[j][:], psum[:, 5, 0:256], AF.Gelu_apprx_tanh, [PB[5]], [B[f"gv{j}"]])
                dve(lambda e, j=j: e.tensor_tensor(gv2[j][:], gv[j][:], gv[j][:], ALU.mult), [B[f"gv{j}"]], [B[f"gv2{j}"]])
                dve(lambda e, j=j: e.tensor_reduce(ss4[j][:, 0:4], gv2[j][:].rearrange("p (g c) -> p g c", g=4), AX.X, ALU.add),
                    [B[f"gv2{j}"]], [B[f"ss4{j}"]])
                act(ss4[j][:, 0:4], ss4[j][:, 0:4], AF.Sqrt, [B[f"ss4{j}"], B["epsb"]], [B[f"ss4{j}"]], bias=epsb[:, 0:1], scale=1.0 / 64.0)
                dve(lambda e, j=j: e.reciprocal(ss4[j][:, 0:4], ss4[j][:, 0:4]), [B[f"ss4{j}"]], [B[f"ss4{j}"]])
                dve(lambda e, j=j: e.tensor_tensor(gv2[j][:].rearrange("p (g c) -> p g c", g=4), gv[j][:].rearrange("p (g c) -> p g c", g=4),
                                                   ss4[j][:, 0:4].unsqueeze(2).to_broadcast([128, 4, 64]), ALU.mult),
                    [B[f"gv{j}"], B[f"ss4{j}"]], [B[f"gv2{j}"]])
                dve(lambda e, j=j, s=s: e.tensor_tensor(vn[:, s, :], gv2[j][:], vnbt[gi][:], ALU.mult),
                    [B[f"gv2{j}"], B[f"vnbt{gi}"]], [B["vn"]])
            def zinfo(m):
                if m < 2:
                    return 24, q[:, m, :], qbuf, False
                if m < 4:
                    return 25, kast[slot][:, m - 2, :], B[f"kast{slot}"], False
                if m < 6:
                    return None
                if m < 10:
                    return 26, q[:, 4 + (m - 6), :], qbuf, True
                return 27, kcst[slot][:], B[f"kcst{slot}"], True

            def pre(m):
                zb = m % 4
                inf = zinfo(m)
                if inf is None:
                    act(ug[:, m - 4, :], psum[:, zb, :], AF.Gelu_apprx_tanh, [PB[zb]], [B["ug"]])
                else:
                    qk_pre(l, m, zb, inf[0], inf[3])

            def post(m):
                inf = zinfo(m)
                if inf is not None:
                    qk_chunk(l, m, m % 4, inf[0], inf[1], inf[2], inf[3], slot)
            for (m0, nm) in [(0, 2), (2, 2), (4, 2), (6, 2), (8, 2), (10, 1)]:
                wv, wvb = wload("win", l, wsrc("win", l, 0, 8, m0 * 128, nm * 128), 8, nm * 128)
                for ml in range(nm):
                    m = m0 + ml
                    zb = m % 4
                    mm([(psum[:, zb, :], wv[:, k, ml * 128:(ml + 1) * 128], hb[:, k, :], k == 0, k == 7, None) for k in range(8)],
                       [wvb, B["hb"]], [PB[zb]])
                    pre(m)
                    if m >= 1:
                        post(m - 1)
            post(10)
            if stop == "a2":
                return
            for o in range(2):
                calls = []
                for s in range(4):
                    for gg in range(2):
                        g = 2 * o + gg
                        calls.append((psum[gg * 64:(gg + 1) * 64, 7, s * 128:(s + 1) * 128], vn[:, s, g * 64:(g + 1) * 64],
                                      sgwb[gi][:, g, :], True, True, (0, gg * 64)))
                mm(calls, [B["vn"], B["sgwb0"]], [PB[7]])
                dve(lambda e, o=o: e.tensor_tensor(tmpc[1][:], psum[:, 7, :], sgbt[gi][:, o, :], ALU.add), [PB[7], B["sgbt0"]], [B["tmpc1"]])
                dve(lambda e, o=o: e.tensor_tensor(q[:, 2 + o, :], tmpc[1][:], ug[:, o, :], ALU.mult), [B["tmpc1"], B["ug"]], [qbuf])
            if stop == "a3":
                return
            def st(e):
                return [e.dma_start(out=QO[:, :, t * T:(t + 1) * T], in_=q[:]),
                        e.dma_start(out=KA[:, :, t * T:(t + 1) * T], in_=kast[slot][:]),
                        e.dma_start(out=KC[:, t * T:(t + 1) * T], in_=kcst[slot][:]),
                        e.dma_start(out=VA[:, 4 * t:4 * t + 4, :], in_=vast[slot][:]),
                        e.dma_start(out=VC[:, 4 * t:4 * t + 4, :], in_=vcst[slot][:])]
            S.dma("pool", f"sa{slot}", st, 5,
                  reads=[qbuf, B[f"kast{slot}"], B[f"kcst{slot}"], B[f"vast{slot}"], B[f"vcst{slot}"]],
                  writes=[B["d_QO"], B["d_KA"], B["d_KC"], B["d_VA"], B["d_VC"]])

        def na_tile(l, t, slot):
            gi = l % 2
            q = qo[slot]
            qbuf = B[f"qo{slot}"]
            wlo = 4 * t - 2
            for s in range(4):
                qb = 4 * t + s
                cls, kbs = na_blocks(qb)
                if cls == 0:
                    bias, biasbuf = nabr[gi], B["nabr0"]
                else:
                    assert ("nab", l) in cast_ready
                    S.dma("sp", "nabe", lambda e, cls=cls: [e.dma_start(out=nabe[:], in_=WB["nab", l][cls])], 1,
                          reads=[DB("nab", l)], writes=[B["nabe"]])
                    bias, biasbuf = nabe, B["nabe"]
                nb = len(kbs)

                def na_qk(h):
                    ch, hh = h // 2, h % 2
                    sb2 = (0, 1) if h % 2 == 0 else (2, 3)
                    pj = h % 2
                    calls = []
                    for j, kb in enumerate(kbs):
                        lb = kb - wlo
                        o_ = psum[:, sb2[0] + j // 4, (j % 4) * 128:(j % 4 + 1) * 128]
                        calls.append((o_, kaw[slot][hh * 64:(hh + 1) * 64, ch, lb * 128:(lb + 1) * 128],
                                      q[hh * 64:(hh + 1) * 64, ch, s * 128:(s + 1) * 128], True, False, (hh * 64, 0)))
                        calls.append((o_, cstb[:, 4, :], bias[:, (h * 5 + j) * 128:(h * 5 + j + 1) * 128], False, True, None))
                    mm(calls, [B["kaw0"], qbuf, B["cstb"], biasbuf], [PB[sb2[0]], PB[sb2[1]]])
                    act(ptn[pj][:, 0:nb, :], psum[:, sb2[0]:sb2[0] + 2, :].rearrange("p b (j q) -> p (b j) q", q=128)[:, 0:nb, :],
                        AF.Exp, [PB[sb2[0]], PB[sb2[1]]], [B[f"ptn{pj}"]], scale=0.125)

                def na_pv(h):
                    ch, hh = h // 2, h % 2
                    pj = h % 2
                    c0 = ch * 192 + (0 if hh == 0 else 64)
                    calls = []
                    for j, kb in enumerate(kbs):
                        lb = kb - wlo
                        calls.append((psum[:, 4, h * 128:(h + 1) * 128], vaw[slot][:, lb, c0:c0 + 128], ptn[pj][:, j, :], j == 0, j == nb - 1, None))
                    mm(calls, [B["vaw0"], B[f"ptn{pj}"]], [PB[4]])
                na_qk(0)
                for h in range(4):
                    if h + 1 < 4:
                        na_qk(h + 1)
                    na_pv(h)
                for h in range(4):
                    ch, hh = h // 2, h % 2
                    num = slice(0, 64) if hh == 0 else slice(64, 128)
                    den = slice(64, 128) if hh == 0 else slice(0, 64)
                    dve(lambda e, h=h, num=num, den=den: e.reciprocal(rc[0][num, h * 128:(h + 1) * 128], psum[den, 4, h * 128:(h + 1) * 128]),
                        [PB[4]], [B["rc0"]])
                    dve(lambda e, h=h, num=num, ch=ch, s=s: e.tensor_tensor(ao[num, ch, s * 128:(s + 1) * 128], psum[num, 4, h * 128:(h + 1) * 128],
                                                                            rc[0][num, h * 128:(h + 1) * 128], ALU.mult),
                        [PB[4], B["rc0"]], [B["ao"]])

        def gqa_tile(l, t, slot, filler=None):
            q = qo[slot]
            qbuf = B[f"qo{slot}"]
            NKC = SEQ // 128
            for c in range(4):
                o4, o5 = (4, 5)
                if filler is not None:
                    next(filler, None)

                def qk(kc):
                    sb2 = (0, 1) if kc % 2 == 0 else (2, 3)
                    mm([(psum[:, sb2[0], :], kcr[0:64, kc * 128:(kc + 1) * 128], q[0:64, 4 + c, :], True, True, (0, 0)),
                        (psum[:, sb2[1], :], kcr[64:128, kc * 128:(kc + 1) * 128], q[64:128, 4 + c, :], True, True, (64, 0))],
                       [B["kcr"], qbuf], [PB[sb2[0]], PB[sb2[1]]])
                    pi = kc % 3
                    act(pt[pi][:], psum[:, sb2[0]:sb2[0] + 2, :], AF.Exp, [PB[sb2[0]], PB[sb2[1]]], [B[f"pt{pi}"]], scale=0.125)

                def pv(kc):
                    pi = kc % 3
                    mm([(psum[:, o4, :], vcr[:, kc, 0:128], pt[pi][:, 0, :], kc == 0, kc == NKC - 1, None),
                        (psum[:, o5, :], vcr[:, kc, 64:192], pt[pi][:, 1, :], kc == 0, kc == NKC - 1, None)],
                       [B["vcr"], B[f"pt{pi}"]], [PB[o4], PB[o5]])
                qk(0)
                qk(1)
                for kc in range(NKC):
                    if kc + 2 < NKC:
                        qk(kc + 2)
                    pv(kc)
                    if filler is not None and kc % 7 == 6:
                        next(filler, None)
                j = c % 2
                dve(lambda e, j=j, o4=o4: e.reciprocal(rc[j][0:64, :], psum[64:128, o4, :]), [PB[o4]], [B[f"rc{j}"]])
                dve(lambda e, c=c, j=j, o4=o4: e.tensor_tensor(ao[0:64, 2 + c, :], psum[0:64, o4, :], rc[j][0:64, :], ALU.mult), [PB[o4], B[f"rc{j}"]], [B["ao"]])
                dve(lambda e, j=j, o5=o5: e.reciprocal(rc[j][64:128, :], psum[0:64, o5, :]), [PB[o5]], [B[f"rc{j}"]])
                dve(lambda e, c=c, j=j, o5=o5: e.tensor_tensor(ao[64:128, 2 + c, :], psum[64:128, o5, :], rc[j][64:128, :], ALU.mult), [PB[o5], B[f"rc{j}"]], [B["ao"]])

        pend = {"gen": None, "fin": None}

        def drain_ffn2():
            if pend["gen"] is not None:
                for _ in pend["gen"]:
                    pass
                pend["fin"]()
                pend["gen"] = None

        def pass_b_tile(l, t, last):
            gi = l % 2
            slot = t % 2
            q, qbuf = qo[slot], B[f"qo{slot}"]
            xt, xbuf = xs[slot], B[f"xs{slot}"]
            S.dma("sp", f"qo{slot}", lambda e: [e.dma_start(out=q[:], in_=QO[:, :, t * T:(t + 1) * T])], 1,
                  reads=[B["d_QO"]], writes=[qbuf])
            lo = max(0, 4 * t - 2)
            hi = min(64, 4 * t + 6)
            off = lo - (4 * t - 2)
            S.dma("sp", "naw", lambda e: [e.dma_start(out=kaw[slot][:, :, off * 128:(off + hi - lo) * 128], in_=KA[:, :, lo * 128:hi * 128]),
                                          e.dma_start(out=vaw[slot][:, off:off + hi - lo, :], in_=VA[:, lo:hi, :])], 2,
                  reads=[B["d_KA"], B["d_VA"]], writes=[B["kaw0"], B["vaw0"]])
            load_x(t, slot, X1, B["d_X1"])
            na_tile(l, t, slot)
            gqa_tile(l, t, slot, filler=pend["gen"])
            drain_ffn2()
            rhs_k = [(ao[:, 0, :], B["ao"]), (ao[:, 1, :], B["ao"]), (q[:, 2, :], qbuf), (q[:, 3, :], qbuf)] + \
                    [(ao[:, 2 + c, :], B["ao"]) for c in range(4)]
            for g in range(4):
                wv, wvb = wload("wo", l, wsrc("wo", l, 0, 8, g * 256, 256), 8, 256)
                for nl in range(2):
                    n = g * 2 + nl
                    zb = n % 4
                    mm([(psum[:, zb, :], wv[:, k, nl * 128:(nl + 1) * 128], rhs_k[k][0], k == 0, k == 7, None) for k in range(8)],
                       [wvb, B["ao"], qbuf], [PB[zb]])
                    dve(lambda e, n=n, zb=zb: e.tensor_tensor(xt[:, n, :], xt[:, n, :], psum[:, zb, :], ALU.add), [PB[zb], xbuf], [xbuf])
            rmsnorm_x(xt, xbuf, 16, gi)
            pend["gen"] = ffn_gen(l, "wg2", "wu2", "wd2", xt, xbuf, gu_banks=((6, 7),), d_banks=(6, 7), tanh_silu=True)
            if last:
                pend["fin"] = lambda: store_x_tokmajor(t, slot)
            else:
                pend["fin"] = lambda: store_x(t, slot, X0, B["d_X0"])

        S.dma("sp", "cst", lambda e: [e.dma_start(out=cst32[:], in_=consts_d[:])], 1, writes=[B["cst32"]])
        dve(lambda e: e.tensor_copy(cstb[:], cst32[:]), [B["cst32"]], [B["cstb"]])
        dve(lambda e: e.memset(epsb[:], EPS), [], [B["epsb"]])
        def init_ones():
            for i in range(2):
                dve(lambda e, i=i: e.memset(vast[i][:], 1.0), [], [B[f"vast{i}"]])
                dve(lambda e, i=i: e.memset(vcst[i][:], 1.0), [], [B[f"vcst{i}"]])
        queue_casts(0, KA_)
        pump_casts(10 ** 9)
        queue_casts(0, KB_)
        for l in range(L):
            gi = l % 2
            if l > 0:
                phase_switch("B", "A")
            init_ones()
            layer_params(l)
            for t in range(ntiles):
                pass_a_tile(l, t, first=(l == 0))
                pump_casts(2)
            if stop in ("ffn1", "a", "a1", "a2", "a3"):
                break
            pump_casts(10 ** 9)
            phase_switch("A", "B")
            S.dma("sp", "nabr", lambda e, l=l, gi=gi: [e.dma_start(out=nabr[gi][:], in_=WB["nab", l][0])], 1,
                  reads=[DB("nab", l)], writes=[B["nabr0"]])
            S.dma("sp", "kvr", lambda e: [e.dma_start(out=kcr[:], in_=KC[:]), e.dma_start(out=vcr[:], in_=VC[:])], 2,
                  reads=[B["d_KC"], B["d_VC"]], writes=[B["kcr"], B["vcr"]])
            if l + 1 < L:
                queue_casts(l + 1, KA_ + KB_)
            for t in range(ntiles if ntiles_b is None else ntiles_b):
                pass_b_tile(l, t, last=(l == L - 1))
                pump_casts(4)
            drain_ffn2()
            pump_casts(10 ** 9)
        outs = [B["d_y"], B["d_X1"], B["d_X0"], B["d_QO"], B["d_KA"], B["d_KC"], B["d_VA"], B["d_VC"]]
        S.final_wait("pool", outs)
        S.final_wait("sp", outs)
        S.emit_all()
        S.close()
    return nc


_CACHE = {}


def kernel(**inputs):
    sh = prep_shared(inputs)
    xp = np.asarray(inputs["x_prompt"], np.float32)
    xsm = np.asarray(inputs["x_sample"], np.float32)
    seqs = [xp[i] for i in range(xp.shape[0])] + [xsm[i] for i in range(xsm.shape[0])]
    assert len(seqs) == NCORES
    if "nc" not in _CACHE:
        _CACHE["nc"] = build()
    nc = _CACHE["nc"]
    in_maps = []
    for s in seqs:
        m = dict(sh)
        m["x"] = np.ascontiguousarray(s)
        in_maps.append(m)
    res = run_bass_kernel_spmd(nc, in_maps, core_ids=list(range(NCORES)))
    ys = [np.asarray(r["y"], np.float32) for r in res.results]
    y_prompt = np.stack(ys[:xp.shape[0]], axis=0)
    y_sample = np.stack(ys[xp.shape[0]:], axis=0)
    return (y_prompt, y_sample)
```

```python
import os
import numpy as np
from contextlib import ExitStack
import concourse.bass as bass
import concourse.mybir as mybir
from concourse.bass_utils import run_bass_kernel_spmd

F32 = mybir.dt.float32
BF16 = mybir.dt.bfloat16
AF = mybir.ActivationFunctionType
ALU = mybir.AluOpType
AX = mybir.AxisListType

SEQ = 8192
D = 1024
FF = 2816
NFF = 22
T = 512
NT = SEQ // T
DEPTH = 4
EPS = 1e-6
NCORES = 6


class Buf:
    __slots__ = ("name", "w", "r", "nowaw", "excl")

    def __init__(self, name, nowaw=False, excl=False):
        self.name = name
        self.w = {}
        self.r = {}
        self.nowaw = nowaw
        self.excl = excl


class Sched:
    ENGS = ("pe", "act", "dve", "pool", "sp")

    def __init__(self, nc):
        self.nc = nc
        self.streams = {e: [] for e in self.ENGS}
        self.cnt = {e: 0 for e in self.ENGS}
        self.waited = {e: {} for e in self.ENGS}
        self.sems = {}
        self.dcum = {}
        self._ctx = []
        self.ninstr = 0

    def sem(self, name):
        if name not in self.sems:
            cm = self.nc.semaphore(name)
            self.sems[name] = cm.__enter__()
            self._ctx.append(cm)
        return name

    def _deps(self, eng, reads, writes, own):
        deps = {}

        def add(d, skip_own):
            for s, v in d.items():
                if skip_own and s == own:
                    continue
                if deps.get(s, 0) < v:
                    deps[s] = v
        for b in reads:
            add(b.w, False)
            if b.excl:
                add(b.r, True)
        skip = (eng == "pe")
        for b in writes:
            if not b.nowaw:
                add(b.w, skip)
            add(b.r, skip)
        out = []
        wt = self.waited[eng]
        for s, v in deps.items():
            if wt.get(s, 0) >= v:
                continue
            wt[s] = v
            out.append((s, v))
        return out

    @staticmethod
    def _commit(tok, reads, writes):
        s, v = tok
        for b in reads:
            if b.r.get(s, 0) < v:
                b.r[s] = v
        for b in writes:
            if b.w.get(s, 0) < v:
                b.w[s] = v

    def op(self, eng, emit, reads=(), writes=()):
        own = self.sem("e_" + eng)
        waits = self._deps(eng, reads, writes, own)
        self.cnt[eng] += 1
        tok = (own, self.cnt[eng])
        self.streams[eng].append((waits, emit, own, 1))
        self._commit(tok, reads, writes)
        return tok

    def dma(self, eng, semname, emit, n, reads=(), writes=()):
        s = self.sem(semname)
        prev = self.dcum.get(s, 0)
        waits = self._deps(eng, reads, writes, None)
        wt = self.waited[eng]
        if prev > 0 and wt.get(s, 0) < prev:
            wt[s] = prev
            waits.append((s, prev))
        cum = prev + 16 * n
        self.dcum[s] = cum
        tok = (s, cum)
        self.streams[eng].append((waits, emit, s, 16))
        self._commit(tok, reads, writes)
        return tok

    def final_wait(self, eng, bufs):
        deps = {}
        for b in bufs:
            for s, v in b.w.items():
                deps[s] = max(deps.get(s, 0), v)
        self.streams[eng].append((list(deps.items()), None, None, 0))

    def emit_all(self):
        nc = self.nc
        sems = self.sems
        streams = self.streams

        def run(engname, eng):
            for waits, emit, s, inc in streams[engname]:
                for ws, wv in waits:
                    eng.wait_ge(sems[ws], wv)
                if emit is None:
                    continue
                r = emit(eng)
                if isinstance(r, (list, tuple)):
                    for ins in r:
                        ins.then_inc(sems[s], inc)
                else:
                    r.then_inc(sems[s], inc)

        with nc.Block() as block:
            @block.tensor
            def _(e):
                run("pe", e)

            @block.scalar
            def _(e):
                run("act", e)

            @block.vector
            def _(e):
                run("dve", e)

            @block.gpsimd
            def _(e):
                run("pool", e)

            @block.sync
            def _(e):
                run("sp", e)

    def close(self):
        for cm in reversed(self._ctx):
            cm.__exit__(None, None, None)


NA_CLASS_QB = (2, 0, 1, 62, 63)


def na_blocks(qb):
    if qb == 0:
        return 1, [0, 1, 2, 3]
    if qb == 1:
        return 2, [0, 1, 2, 3]
    if qb == 62:
        return 3, [60, 61, 62, 63]
    if qb == 63:
        return 4, [60, 61, 62, 63]
    return 0, [qb - 2, qb - 1, qb, qb + 1, qb + 2]


def na_index_table():
    idx = np.full((5, 128, 5, 128), 465, np.int64)
    key = np.arange(128)
    qry = np.arange(128)
    for cls, qb in enumerate(NA_CLASS_QB):
        _, kbs = na_blocks(qb)
        qr = 2 * qb + qry // 64
        qc = qry % 64
        rs = np.clip(qr - 4, 0, 120)
        cs = np.clip(qc - 8, 0, 48)
        for j, kb in enumerate(kbs):
            kr = 2 * kb + key // 64
            kc = key % 64
            inwin = ((kr[:, None] >= rs[None, :]) & (kr[:, None] < rs[None, :] + 8) &
                     (kc[:, None] >= cs[None, :]) & (kc[:, None] < cs[None, :] + 16))
            flat = (kr[:, None] - qr[None, :] + 7) * 31 + (kc[:, None] - qc[None, :] + 15)
            idx[cls, :, j, :] = np.where(inwin, flat, 465)
    return idx


def rope_tables():
    t = np.arange(SEQ)
    row = (t // 64).astype(np.float32)
    col = (t % 64).astype(np.float32)
    inv = (np.float32(10000.0) ** (-np.arange(16, dtype=np.float32) / np.float32(16))).astype(np.float32)
    ang = np.concatenate([row[:, None] * inv, col[:, None] * inv], axis=-1).astype(np.float32)
    c = np.cos(ang).astype(np.float32).T
    s = np.sin(ang).astype(np.float32).T
    C = np.concatenate([c, c, c, c], axis=0)
    Ssig = np.concatenate([-s, s, -s, s], axis=0)
    return np.ascontiguousarray(C), np.ascontiguousarray(Ssig)


def const_mats():
    cm = np.zeros((128, 5, 128), np.float32)
    cm[:, 0, :] = np.eye(128)
    cm[:, 1, :] = 1.0 / 1024.0
    for b in range(2):
        cm[b * 64:(b + 1) * 64, 2, b * 64:(b + 1) * 64] = 1.0 / 64.0
    m = np.arange(128)
    perm = (m // 64) * 64 + ((m % 64) + 32) % 64
    cm[perm, 3, m] = 1.0
    cm[:, 4, :] = 8.0 * np.eye(128)
    return cm


def prep_shared(inp):
    L = DEPTH
    h64 = lambda a, i: np.arange(a + 64 * i, a + 64 * (i + 1))
    cols = [np.arange(0, 256), np.arange(256, 512), np.arange(768, 1024)]
    for c in range(4):
        cols += [h64(1280, c), h64(1280, 4 + c)]
    cols += [np.arange(1792, 1920), np.arange(512, 768), np.arange(1920, 2048), np.arange(1024, 1280)]
    cols = np.concatenate(cols)
    assert cols.size == 2048 and np.unique(cols).size == 2048
    rows = [np.arange(0, 512)]
    for c in range(4):
        rows += [h64(512, c), h64(512, 4 + c)]
    rows = np.concatenate(rows)
    f = lambda k: np.asarray(inp[k], np.float32)
    sh = {}
    sh["wg1"] = np.ascontiguousarray(f("ffn1_w_gate"))
    sh["wu1"] = np.ascontiguousarray(f("ffn1_w_up"))
    sh["wd1"] = np.ascontiguousarray(f("ffn1_w_down"))
    sh["wg2"] = np.ascontiguousarray(f("ffn2_w_gate"))
    sh["wu2"] = np.ascontiguousarray(f("ffn2_w_up"))
    sh["wd2"] = np.ascontiguousarray(f("ffn2_w_down"))
    sh["win"] = np.ascontiguousarray(f("w_in")[:, :, cols])
    sh["wo"] = np.ascontiguousarray(f("w_out")[:, rows, :])
    g = np.zeros((L, 128, 28), np.float32)
    for i, k in enumerate(("ffn1_norm", "mix_norm", "ffn2_norm")):
        g[:, :, 8 * i:8 * i + 8] = f(k).reshape(L, 8, 128).transpose(0, 2, 1)
    for i, k in enumerate(("na_q_norm", "na_k_norm", "gqa_q_norm", "gqa_k_norm")):
        g[:, :, 24 + i] = np.tile(f(k), (1, 2))
    sh["gains"] = g
    sh["vnb"] = np.ascontiguousarray(np.broadcast_to(f("sg_v_norm").reshape(L, 1, 256), (L, 128, 256)))
    sb = f("sg_b")
    sgb = np.zeros((L, 128, 2, 4, 128), np.float32)
    for o in range(2):
        sgb[:, 0:64, o] = sb[:, 2 * o][:, None, None, :]
        sgb[:, 64:128, o] = sb[:, 2 * o + 1][:, None, None, :]
    sh["sgb"] = sgb.reshape(L, 128, 2, 512)
    sh["sgw"] = np.ascontiguousarray(f("sg_w").transpose(0, 3, 1, 2))
    rpb = f("na_rpb").reshape(L, 4, 465)
    rpb_ext = np.concatenate([rpb, np.full((L, 4, 1), -30000.0, np.float32)], axis=-1)
    idx = na_index_table()
    nab = rpb_ext[:, :, idx]
    sh["nab"] = np.ascontiguousarray(nab.transpose(0, 2, 3, 1, 4, 5)).reshape(L, 5, 128, 2560)
    C, Ssig = rope_tables()
    sh["ropeC"] = C
    sh["ropeS"] = Ssig
    sh["consts"] = const_mats()
    return sh


def build(nlayers=DEPTH, debug=False, stop=None, ntiles=NT, ntiles_b=None):
    nc = bass.Bass("TRN2", target_bir_lowering=False)
    L = nlayers
    dt_in = lambda name, shape: nc.dram_tensor(name, shape, F32, kind="ExternalInput").ap()
    skind = "ExternalOutput" if debug else "Internal"
    scr = lambda name, shape, dt: nc.dram_tensor(name, shape, dt, kind=skind).ap()

    x_in = dt_in("x", [SEQ, D])
    W32 = {k: dt_in(k, [DEPTH, D, FF]) for k in ("wg1", "wu1", "wg2", "wu2")}
    W32["wd1"] = dt_in("wd1", [DEPTH, FF, D])
    W32["wd2"] = dt_in("wd2", [DEPTH, FF, D])
    W32["win"] = dt_in("win", [DEPTH, D, 2048])
    W32["wo"] = dt_in("wo", [DEPTH, D, D])
    gains_d = dt_in("gains", [DEPTH, 128, 28])
    vnb_d = dt_in("vnb", [DEPTH, 128, 256])
    sgb_d = dt_in("sgb", [DEPTH, 128, 2, 512])
    sgw_d = dt_in("sgw", [DEPTH, 128, 4, 128])
    nab_d = dt_in("nab", [DEPTH, 5, 128, 2560])
    ropeC_d = dt_in("ropeC", [128, SEQ])
    ropeS_d = dt_in("ropeS", [128, SEQ])
    consts_d = dt_in("consts", [128, 5, 128])
    y_out = nc.dram_tensor("y", [SEQ, D], F32, kind="ExternalOutput").ap()

    X1 = scr("X1", [128, 8, SEQ], F32)
    X0 = scr("X0", [128, 8, SEQ], F32)
    QO = scr("QO", [128, 8, SEQ], BF16)
    KA = scr("KA", [128, 2, SEQ], BF16)
    KC = scr("KC", [128, SEQ], BF16)
    VA = scr("VA", [128, 64, 384], BF16)
    VC = scr("VC", [128, 64, 192], BF16)
    WB = {}
    for l in range(L):
        for k in ("wg1", "wu1", "wg2", "wu2"):
            WB[k, l] = nc.dram_tensor(f"{k}b{l}", [D, FF], BF16).ap()
        for k in ("wd1", "wd2"):
            WB[k, l] = nc.dram_tensor(f"{k}b{l}", [FF, D], BF16).ap()
        WB["win", l] = nc.dram_tensor(f"winb{l}", [D, 2048], BF16).ap()
        WB["wo", l] = nc.dram_tensor(f"wob{l}", [D, D], BF16).ap()
        WB["nab", l] = nc.dram_tensor(f"nabb{l}", [5, 128, 2560], BF16).ap()

    S = Sched(nc)
    B = {}

    def buf(name, nowaw=False, excl=False):
        B[name] = Buf(name, nowaw, excl)
        return B[name]

    for n in ("X0", "X1", "QO", "KA", "KC", "VA", "VC", "y"):
        buf("d_" + n, nowaw=True)
    for key in WB:
        buf(f"d_{key[0]}{key[1]}", nowaw=True)
    DB = lambda k, l: B[f"d_{k}{l}"]

    with ExitStack() as es:
        def sb(name, shape, dt):
            t = es.enter_context(nc.sbuf_tensor(name, shape, dt))
            buf(name)
            return t

        psum = es.enter_context(nc.psum_tensor("psum", [128, 8, 512], F32))
        PB = [buf(f"ps{i}", excl=True) for i in range(8)]

        cst32 = sb("cst32", [128, 5, 128], F32)
        cstb = sb("cstb", [128, 5, 128], BF16)
        epsb = sb("epsb", [128, 1], F32)
        gsb = [sb(f"gsb{i}", [128, 28], F32) for i in range(2)]
        xs = [sb(f"xs{i}", [128, 8, T], F32) for i in range(2)]
        hb = sb("hb", [128, 8, T], BF16)
        ab = sb("ab", [128, NFF, T], BF16)
        NW = 6
        wsl = [sb(f"wsl{i}", [128, 2048], BF16) for i in range(NW)]
        qo = [sb(f"qo{i}", [128, 8, T], BF16) for i in range(2)]
        sq = [sb(f"sq{i}", [128, T], BF16) for i in range(2)]
        rstd = [sb(f"rstd{i}", [128, T], F32) for i in range(2)]
        tmpa = [sb(f"tmpa{i}", [128, T], F32) for i in range(2)]
        OVL = 80 * 1024 // 2
        ovl = es.enter_context(nc.sbuf_tensor("ovl", [128, OVL], BF16))
        cur = {"A": 0, "B": 0}
        pool_bufs = {"A": [], "B": []}

        def carve(ph, name, shape, dt):
            n = 1
            for d_ in shape[1:]:
                n *= d_
            nel = n * (2 if dt == F32 else 1)
            nel = (nel + 15) // 16 * 16
            a0 = cur[ph]
            cur[ph] += nel
            assert cur[ph] <= OVL, (ph, name, cur[ph])
            ap = ovl[:, a0:a0 + n * (2 if dt == F32 else 1)]
            if dt == F32:
                ap = ap.bitcast(F32)
            if len(shape) == 3:
                ap = ap.rearrange("p (a b) -> p a b", a=shape[1])
            pool_bufs[ph].append(buf(name))
            return ap

        vnbt = [carve("A", f"vnbt{i}", [128, 256], F32) for i in range(2)]
        sgbt = [carve("A", f"sgbt{i}", [128, 2, 512], F32) for i in range(1)] * 2
        sgw32 = carve("A", "sgw32", [128, 4, 128], F32)
        sgwb = [carve("A", f"sgwb{i}", [128, 4, 128], BF16) for i in range(1)] * 2
        tmpb = [carve("A", f"tmpb{i}", [128, T], F32) for i in range(1)]
        tmpc = [carve("A", f"tmpc{i}", [128, T], F32) for i in range(2)]
        qnb = [carve("A", f"qnb{i}", [128, T], BF16) for i in range(2)]
        ropeCt = [carve("A", f"ropeC{i}", [128, T], F32) for i in range(2)]
        ropeSt = [carve("A", f"ropeS{i}", [128, T], F32) for i in range(2)]
        ug = carve("A", "ug", [128, 2, T], F32)
        gv4 = carve("A", "gv4", [128, 4, 256], F32)
        gv2 = [carve("A", f"gv2{i}", [128, 256], F32) for i in range(2)]
        ss4 = [carve("A", f"ss4{i}", [128, 16], F32) for i in range(2)]
        vn = carve("A", "vn", [128, 4, 256], BF16)
        kast = [carve("A", f"kast{i}", [128, 2, T], BF16) for i in range(2)]
        kcst = [carve("A", f"kcst{i}", [128, T], BF16) for i in range(2)]
        vast = [carve("A", f"vast{i}", [128, 4, 384], BF16) for i in range(2)]
        vcst = [carve("A", f"vcst{i}", [128, 4, 192], BF16) for i in range(2)]
        kcr = carve("B", "kcr", [128, SEQ], BF16)
        vcr = carve("B", "vcr", [128, 64, 192], BF16)
        kaw = [carve("B", f"kaw{i}", [128, 2, 1024], BF16) for i in range(1)] * 2
        vaw = [carve("B", f"vaw{i}", [128, 8, 384], BF16) for i in range(1)] * 2
        nabr = [carve("B", f"nabr{i}", [128, 2560], BF16) for i in range(1)] * 2
        nabe = carve("B", "nabe", [128, 2560], BF16)
        pt = [carve("B", f"pt{i}", [128, 2, T], BF16) for i in range(3)]
        ptn = [carve("B", f"ptn{i}", [128, 5, 128], BF16) for i in range(2)]
        ao = carve("B", "ao", [128, 6, T], BF16)
        rc = [carve("B", f"rc{i}", [128, T], F32) for i in range(2)]

        def phase_switch(old, new):
            allw = {}
            for b_ in pool_bufs[old]:
                for d_ in (b_.w, b_.r):
                    for s_, v_ in d_.items():
                        if allw.get(s_, 0) < v_:
                            allw[s_] = v_
            for b_ in pool_bufs[new]:
                for s_, v_ in allw.items():
                    if b_.w.get(s_, 0) < v_:
                        b_.w[s_] = v_
                    if b_.r.get(s_, 0) < v_:
                        b_.r[s_] = v_

        ctr = {"w": 0}

        def mm(calls, reads, writes):
            def emit(e, calls=calls):
                r = None
                for (o, l_, rh, st, sp, tp) in calls:
                    if tp is None:
                        r = e.matmul(o, lhsT=l_, rhs=rh, start=st, stop=sp)
                    else:
                        r = e.matmul(o, lhsT=l_, rhs=rh, start=st, stop=sp, tile_position=tp)
                return r
            S.op("pe", emit, reads=reads, writes=writes)

        def act(out, in_, func, reads, writes, **kw):
            S.op("act", lambda e: e.activation(out, in_, func, **kw), reads=reads, writes=writes)

        def dve(fn, reads, writes, eng="dve"):
            S.op(eng, fn, reads=reads, writes=writes)

        def wload(key, l, src3, k, c):
            assert (key, l) in cast_ready, (key, l)
            i = ctr["w"] % NW
            ctr["w"] += 1
            view = wsl[i][:, 0:k * c].rearrange("p (k c) -> p k c", k=k)
            S.dma("sp", f"w{i}", lambda e: [e.dma_start(out=view, in_=src3)], 1,
                  reads=[DB(key, l)], writes=[B[f"wsl{i}"]])
            return view, B[f"wsl{i}"]

        def wsrc(key, l, r0, nk, c0, c):
            return WB[key, l].rearrange("(k p) c -> p k c", p=128)[:, r0:r0 + nk, c0:c0 + c]

        pending = []
        cast_ready = set()
        ctr["cast"] = 0

        def queue_casts(l, keys):
            def one(key, l, dst, src):
                def go():
                    i = ctr["cast"] % 4
                    ctr["cast"] += 1
                    S.dma("pool", f"cast{i}", lambda e: [e.dma_start(out=dst, in_=src)], 1, reads=[], writes=[DB(key, l)])
                return go
            for key in keys:
                if key == "nab":
                    pending.append((None, one("nab", l, WB["nab", l].rearrange("c p f -> (c p) f"), nab_d[l].rearrange("c p f -> (c p) f"))))
                else:
                    nrows = FF if key in ("wd1", "wd2") else D
                    for r0 in range(0, nrows, 256):
                        r1 = min(nrows, r0 + 256)
                        pending.append((None, one(key, l, WB[key, l][r0:r1, :], W32[key][l][r0:r1, :])))
                pending.append(((key, l), None))

        def pump_casts(n):
            while pending and n > 0:
                mark, go = pending.pop(0)
                if go is None:
                    cast_ready.add(mark)
                else:
                    go()
                    n -= 1
            while pending and pending[0][1] is None:
                cast_ready.add(pending.pop(0)[0])

        KA_ = ("wg1", "wu1", "wd1", "win")
        KB_ = ("nab", "wo", "wg2", "wu2", "wd2")

        def layer_params(l):
            i = l % 2
            S.dma("sp", f"lp{i}", lambda e: [e.dma_start(out=gsb[i][:], in_=gains_d[l]),
                                             e.dma_start(out=vnbt[i][:], in_=vnb_d[l]),
                                             e.dma_start(out=sgbt[i][:], in_=sgb_d[l])], 3,
                  writes=[B[f"gsb{i}"], B[f"vnbt{i}"], B["sgbt0"]])
            S.dma("sp", "lpw", lambda e: [e.dma_start(out=sgw32[:], in_=sgw_d[l])], 1, writes=[B["sgw32"]])
            dve(lambda e: e.tensor_copy(sgwb[i][:], sgw32[:]), [B["sgw32"]], [B["sgwb0"]])


        def rmsnorm_x(xt, xbuf, gcol0, gi):
            calls = []
            for c in range(8):
                j = c % 2
                act(sq[j][:], xt[:, c, :], AF.Square, [xbuf], [B[f"sq{j}"]])
                mm([(psum[:, 6, :], cstb[:, 1, :], sq[j][:], c == 0, c == 7, None)], [B[f"sq{j}"], B["cstb"]], [PB[6]])
            act(rstd[0][:], psum[:, 6, :], AF.Ln, [PB[6], B["epsb"]], [B["rstd0"]], bias=epsb[:, 0:1], scale=1.0)
            act(rstd[0][:], rstd[0][:], AF.Exp, [B["rstd0"]], [B["rstd0"]], scale=-0.5)
            for c in range(8):
                dve(lambda e, c=c: e.scalar_tensor_tensor(hb[:, c, :], xt[:, c, :], gsb[gi][:, gcol0 + c:gcol0 + c + 1],
                                                           rstd[0][:], ALU.mult, ALU.mult),
                    [xbuf, B["rstd0"], B[f"gsb{gi}"]], [B["hb"]])

        def ffn_gen(l, kg, ku, kd, xt, xbuf, gu_banks=((0, 1), (2, 3)), d_banks=(4, 5), tanh_silu=False):
            groups = [(2 * i, 2) for i in range(11)]
            for (m0, nm) in groups:
                wg, wgb = wload(kg, l, wsrc(kg, l, 0, 8, m0 * 128, nm * 128), 8, nm * 128)
                wu, wub = wload(ku, l, wsrc(ku, l, 0, 8, m0 * 128, nm * 128), 8, nm * 128)
                for ml in range(nm):
                    m = m0 + ml
                    bg, bu = gu_banks[m % len(gu_banks)]
                    mm([(psum[:, bg, :], wg[:, k, ml * 128:(ml + 1) * 128], hb[:, k, :], k == 0, k == 7, None) for k in range(8)],
                       [wgb, B["hb"]], [PB[bg]])
                    mm([(psum[:, bu, :], wu[:, k, ml * 128:(ml + 1) * 128], hb[:, k, :], k == 0, k == 7, None) for k in range(8)],
                       [wub, B["hb"]], [PB[bu]])
                    j = m % 2
                    if tanh_silu:
                        act(tmpa[j][:], psum[:, bg, :], AF.Tanh, [PB[bg]], [B[f"tmpa{j}"]], scale=0.5)
                        dve(lambda e, j=j, bg=bg: e.scalar_tensor_tensor(tmpa[j][:], tmpa[j][:], 1.0, psum[:, bg, :], ALU.add, ALU.mult),
                            [B[f"tmpa{j}"], PB[bg]], [B[f"tmpa{j}"]])
                    else:
                        act(tmpa[j][:], psum[:, bg, :], AF.Silu, [PB[bg]], [B[f"tmpa{j}"]])
                    dve(lambda e, m=m, j=j, bu=bu: e.tensor_tensor(ab[:, m, :], tmpa[j][:], psum[:, bu, :], ALU.mult),
                        [B[f"tmpa{j}"], PB[bu]], [B["ab"]])
                    yield
            rs = 0.25 if tanh_silu else 0.5
            for npair in range(4):
                for (mr0, nmr) in ((0, 8), (8, 8), (16, 6)):
                    src = WB[kd, l].rearrange("(m p) c -> p m c", p=128)[:, mr0:mr0 + nmr, npair * 256:npair * 256 + 256]
                    wd, wdb = wload(kd, l, src, nmr, 256)
                    for nn in range(2):
                        mm([(psum[:, d_banks[nn], :], wd[:, mi, nn * 128:(nn + 1) * 128], ab[:, mr0 + mi, :],
                             mr0 + mi == 0, mr0 + mi == NFF - 1, None) for mi in range(nmr)],
                           [wdb, B["ab"]], [PB[d_banks[nn]]])
                    if mr0 < 16:
                        yield
                for nn in range(2):
                    n = npair * 2 + nn
                    dve(lambda e, n=n, nn=nn: e.scalar_tensor_tensor(xt[:, n, :], psum[:, d_banks[nn], :], rs, xt[:, n, :], ALU.mult, ALU.add),
                        [PB[d_banks[nn]], xbuf], [xbuf])
                yield

        def ffn(l, kg, ku, kd, xt, xbuf):
            for _ in ffn_gen(l, kg, ku, kd, xt, xbuf):
                pass

        def load_x(t, slot, src, srcbuf):
            S.dma("sp", f"xs{slot}", lambda e: [e.dma_start(out=xs[slot][:], in_=src[:, :, t * T:(t + 1) * T])], 1,
                  reads=[srcbuf], writes=[B[f"xs{slot}"]])

        def load_x_tokmajor(t, slot):
            for b in range(4):
                i = ctr["w"] % NW
                ctr["w"] += 1
                tmv = wsl[i][:, 0:2048].bitcast(F32)
                r0 = t * T + b * 128
                S.dma("sp", f"w{i}", lambda e, tmv=tmv, r0=r0: [e.dma_start(out=tmv, in_=x_in[r0:r0 + 128, :])], 1,
                      writes=[B[f"wsl{i}"]])
                for hf in range(2):
                    def emit(e, tmv=tmv, hf=hf):
                        r = None
                        for c4 in range(4):
                            c = hf * 4 + c4
                            r = e.transpose(psum[:, hf, c4 * 128:(c4 + 1) * 128], tmv[:, c * 128:(c + 1) * 128], cst32[:, 0, :])
                        return r
                    S.op("pe", emit, reads=[B[f"wsl{i}"], B["cst32"]], writes=[PB[hf]])
                    dve(lambda e, hf=hf, b=b: e.tensor_copy(xs[slot][:, hf * 4:hf * 4 + 4, b * 128:(b + 1) * 128],
                                                            psum[:, hf, :].rearrange("p (c t) -> p c t", c=4)),
                        [PB[hf]], [B[f"xs{slot}"]])

        def store_x(t, slot, dst, dstbuf):
            S.dma("pool", f"sx{slot}", lambda e: [e.dma_start(out=dst[:, :, t * T:(t + 1) * T], in_=xs[slot][:])], 1,
                  reads=[B[f"xs{slot}"]], writes=[dstbuf])

        def store_x_tokmajor(t, slot):
            for b in range(4):
                i = ctr["w"] % NW
                ctr["w"] += 1
                tmv = wsl[i][:, 0:2048].bitcast(F32)
                for hf in range(2):
                    def emit(e, hf=hf, b=b):
                        r = None
                        for c4 in range(4):
                            c = hf * 4 + c4
                            r = e.transpose(psum[:, hf, c4 * 128:(c4 + 1) * 128], xs[slot][:, c, b * 128:(b + 1) * 128], cst32[:, 0, :])
                        return r
                    S.op("pe", emit, reads=[B[f"xs{slot}"], B["cst32"]], writes=[PB[hf]])
                    dve(lambda e, hf=hf, tmv=tmv: e.tensor_copy(tmv[:, hf * 512:(hf + 1) * 512], psum[:, hf, :]),
                        [PB[hf]], [B[f"wsl{i}"]])
                r0 = t * T + b * 128
                S.dma("pool", f"sy{i}", lambda e, tmv=tmv, r0=r0: [e.dma_start(out=y_out[r0:r0 + 128, :], in_=tmv)], 1,
                      reads=[B[f"wsl{i}"]], writes=[B["d_y"]])

        def qk_pre(l, m, zb, gcol, rope):
            gi = l % 2
            j = m % 2
            act(sq[j][:], psum[:, zb, :], AF.Square, [PB[zb]], [B[f"sq{j}"]])
            if rope:
                act(qnb[j][:], psum[:, zb, :], AF.Copy, [PB[zb], B[f"gsb{gi}"]], [B[f"qnb{j}"]], scale=gsb[gi][:, gcol:gcol + 1])

        def qk_chunk(l, m, zb, gcol, dst, dstbuf, rope, tslot):
            gi = l % 2
            j = m % 2
            mm([(psum[:, 6, :], cstb[:, 2, :], sq[j][:], True, True, None)], [B[f"sq{j}"], B["cstb"]], [PB[6]])
            if rope:
                mm([(psum[:, 7, :], cstb[:, 3, :], qnb[j][:], True, True, None)], [B[f"qnb{j}"], B["cstb"]], [PB[7]])
            act(rstd[1][:], psum[:, 6, :], AF.Ln, [PB[6], B["epsb"]], [B["rstd1"]], bias=epsb[:, 0:1], scale=1.0)
            act(rstd[1][:], rstd[1][:], AF.Exp, [B["rstd1"]], [B["rstd1"]], scale=-0.5)
            if not rope:
                dve(lambda e: e.scalar_tensor_tensor(dst, psum[:, zb, :], gsb[gi][:, gcol:gcol + 1], rstd[1][:], ALU.mult, ALU.mult),
                    [PB[zb], B["rstd1"], B[f"gsb{gi}"]], [dstbuf])
                return
            dve(lambda e: e.scalar_tensor_tensor(tmpb[0][:], psum[:, zb, :], gsb[gi][:, gcol:gcol + 1], ropeCt[tslot][:], ALU.mult, ALU.mult),
                [PB[zb], B[f"gsb{gi}"], B[f"ropeC{tslot}"]], [B["tmpb0"]])
            dve(lambda e: e.tensor_tensor(tmpc[0][:], psum[:, 7, :], ropeSt[tslot][:], ALU.mult), [PB[7], B[f"ropeS{tslot}"]], [B["tmpc0"]])
            dve(lambda e: e.tensor_tensor(tmpb[0][:], tmpb[0][:], tmpc[0][:], ALU.add), [B["tmpb0"], B["tmpc0"]], [B["tmpb0"]])
            dve(lambda e: e.tensor_tensor(dst, tmpb[0][:], rstd[1][:], ALU.mult), [B["tmpb0"], B["rstd1"]], [dstbuf])

        def pass_a_tile(l, t, first):
            gi = l % 2
            slot = t % 2
            xt, xbuf = xs[slot], B[f"xs{slot}"]
            if first:
                load_x_tokmajor(t, slot)
            else:
                load_x(t, slot, X0, B["d_X0"])
            S.dma("sp", f"rope{slot}", lambda e: [e.dma_start(out=ropeCt[slot][:], in_=ropeC_d[:, t * T:(t + 1) * T]),
                                                  e.dma_start(out=ropeSt[slot][:], in_=ropeS_d[:, t * T:(t + 1) * T])], 2,
                  writes=[B[f"ropeC{slot}"], B[f"ropeS{slot}"]])
            rmsnorm_x(xt, xbuf, 0, gi)
            ffn(l, "wg1", "wu1", "wd1", xt, xbuf)
            rmsnorm_x(xt, xbuf, 8, gi)
            store_x(t, slot, X1, B["d_X1"])
            if stop == "ffn1":
                return
            q, qbuf = qo[slot], B[f"qo{slot}"]
            wt1a, wt1ab = wload("win", l, wsrc("win", l, 0, 4, 1408, 384), 4, 384)
            wt1c, wt1cb = wload("win", l, wsrc("win", l, 4, 4, 1408, 384), 4, 384)
            wt2, wt2b = wload("win", l, wsrc("win", l, 0, 8, 1792, 256), 8, 256)
            for s in range(4):
                mm([(psum[:, 4, 0:384], hb[:, k, s * 128:(s + 1) * 128], (wt1a if k < 4 else wt1c)[:, k % 4, :], k == 0, k == 7, None) for k in range(8)],
                   [wt1ab, wt1cb, B["hb"]], [PB[4]])
                mm([(psum[:, 5, 0:256], hb[:, k, s * 128:(s + 1) * 128], wt2[:, k, :], k == 0, k == 7, None) for k in range(8)],
                   [wt2b, B["hb"]], [PB[5]])
                va_dst = vast[slot]
                def cpva(e, s=s, va_dst=va_dst):
                    e.tensor_copy(va_dst[:, s, 0:64], psum[:, 4, 0:64])
                    e.tensor_copy(va_dst[:, s, 128:256], psum[:, 4, 64:192])
                    e.tensor_copy(va_dst[:, s, 320:384], psum[:, 4, 192:256])
                    e.tensor_copy(vcst[slot][:, s, 0:64], psum[:, 4, 256:320])
                    return e.tensor_copy(vcst[slot][:, s, 128:192], psum[:, 4, 320:384])
                dve(cpva, [PB[4]], [B[f"vast{slot}"], B[f"vcst{slot}"]])
                act(gv4[:, s, :], psum[:, 5, 0:256], AF.Gelu_apprx_tanh, [PB[5]], [B["gv4"]])
                j = s % 2
                dve(lambda e, j=j, s=s: e.tensor_tensor(gv2[j][:], gv4[:, s, :], gv4[:, s, :], ALU.mult), [B["gv4"]], [B[f"gv2{j}"]])
                dve(lambda e, j=j, s=s: e.tensor_reduce(ss4[0][:, 4 * s:4 * s + 4], gv2[j][:].rearrange("p (g c) -> p g c", g=4), AX.X, ALU.add),
                    [B[f"gv2{j}"]], [B["ss40"]])
            act(ss4[0][:], ss4[0][:], AF.Ln, [B["ss40"], B["epsb"]], [B["ss40"]], bias=epsb[:, 0:1], scale=1.0 / 64.0)
            act(ss4[0][:], ss4[0][:], AF.Exp, [B["ss40"]], [B["ss40"]], scale=-0.5)
            for s in range(4):
                j = s % 2
                dve(lambda e, j=j, s=s: e.tensor_tensor(gv2[j][:].rearrange("p (g c) -> p g c", g=4), gv4[:, s, :].rearrange("p (g c) -> p g c", g=4),
                                                        ss4[0][:, 4 * s:4 * s + 4].unsqueeze(2).to_broadcast([128, 4, 64]), ALU.mult),
                    [B["gv4"], B["ss40"]], [B[f"gv2{j}"]])
                dve(lambda e, j=j, s=s: e.tensor_tensor(vn[:, s, :], gv2[j][:], vnbt[gi][:], ALU.mult),
                    [B[f"gv2{j}"], B[f"vnbt{gi}"]], [B["vn"]])
            def zinfo(m):
                if m < 2:
                    return 24, q[:, m, :], qbuf, False
                if m < 4:
                    return 25, kast[slot][:, m - 2, :], B[f"kast{slot}"], False
                if m < 6:
                    return None
                if m < 10:
                    return 26, q[:, 4 + (m - 6), :], qbuf, True
                return 27, kcst[slot][:], B[f"kcst{slot}"], True

            def pre(m, i):
                zb = i % 4
                inf = zinfo(m)
                if inf is None:
                    act(ug[:, m - 4, :], psum[:, zb, :], AF.Gelu_apprx_tanh, [PB[zb]], [B["ug"]])
                else:
                    qk_pre(l, i, zb, inf[0], inf[3])

            def post(m, i):
                inf = zinfo(m)
                if inf is not None:
                    qk_chunk(l, i, i % 4, inf[0], inf[1], inf[2], inf[3], slot)
            prev = None
            i = 0
            for (m0, nm) in [(4, 2), (0, 2), (2, 2), (6, 2), (8, 2), (10, 1)]:
                wv, wvb = wload("win", l, wsrc("win", l, 0, 8, m0 * 128, nm * 128), 8, nm * 128)
                for ml in range(nm):
                    m = m0 + ml
                    zb = i % 4
                    mm([(psum[:, zb, :], wv[:, k, ml * 128:(ml + 1) * 128], hb[:, k, :], k == 0, k == 7, None) for k in range(8)],
                       [wvb, B["hb"]], [PB[zb]])
                    pre(m, i)
                    if prev is not None:
                        post(*prev)
                    prev = (m, i)
                    i += 1
            post(*prev)
            if stop == "a2":
                return
            for o in range(2):
                calls = []
                for s in range(4):
                    for gg in range(2):
                        g = 2 * o + gg
                        calls.append((psum[gg * 64:(gg + 1) * 64, 7, s * 128:(s + 1) * 128], vn[:, s, g * 64:(g + 1) * 64],
                                      sgwb[gi][:, g, :], True, True, (0, gg * 64)))
                mm(calls, [B["vn"], B["sgwb0"]], [PB[7]])
                dve(lambda e, o=o: e.tensor_tensor(tmpc[1][:], psum[:, 7, :], sgbt[gi][:, o, :], ALU.add), [PB[7], B["sgbt0"]], [B["tmpc1"]])
                dve(lambda e, o=o: e.tensor_tensor(q[:, 2 + o, :], tmpc[1][:], ug[:, o, :], ALU.mult), [B["tmpc1"], B["ug"]], [qbuf])
            if stop == "a3":
                return
            def st(e):
                return [e.dma_start(out=QO[:, :, t * T:(t + 1) * T], in_=q[:]),
                        e.dma_start(out=KA[:, :, t * T:(t + 1) * T], in_=kast[slot][:]),
                        e.dma_start(out=KC[:, t * T:(t + 1) * T], in_=kcst[slot][:]),
                        e.dma_start(out=VA[:, 4 * t:4 * t + 4, :], in_=vast[slot][:]),
                        e.dma_start(out=VC[:, 4 * t:4 * t + 4, :], in_=vcst[slot][:])]
            S.dma("pool", f"sa{slot}", st, 5,
                  reads=[qbuf, B[f"kast{slot}"], B[f"kcst{slot}"], B[f"vast{slot}"], B[f"vcst{slot}"]],
                  writes=[B["d_QO"], B["d_KA"], B["d_KC"], B["d_VA"], B["d_VC"]])

        def na_tile(l, t, slot):
            gi = l % 2
            q = qo[slot]
            qbuf = B[f"qo{slot}"]
            wlo = 4 * t - 2
            for s in range(4):
                qb = 4 * t + s
                cls, kbs = na_blocks(qb)
                if cls == 0:
                    bias, biasbuf = nabr[gi], B["nabr0"]
                else:
                    assert ("nab", l) in cast_ready
                    S.dma("sp", "nabe", lambda e, cls=cls: [e.dma_start(out=nabe[:], in_=WB["nab", l][cls])], 1,
                          reads=[DB("nab", l)], writes=[B["nabe"]])
                    bias, biasbuf = nabe, B["nabe"]
                nb = len(kbs)

                def na_qk(h):
                    ch, hh = h // 2, h % 2
                    sb2 = (0, 1) if h % 2 == 0 else (2, 3)
                    pj = h % 2
                    calls = []
                    for j, kb in enumerate(kbs):
                        lb = kb - wlo
                        o_ = psum[:, sb2[0] + j // 4, (j % 4) * 128:(j % 4 + 1) * 128]
                        calls.append((o_, kaw[slot][hh * 64:(hh + 1) * 64, ch, lb * 128:(lb + 1) * 128],
                                      q[hh * 64:(hh + 1) * 64, ch, s * 128:(s + 1) * 128], True, False, (hh * 64, 0)))
                        calls.append((o_, cstb[:, 4, :], bias[:, (h * 5 + j) * 128:(h * 5 + j + 1) * 128], False, True, None))
                    mm(calls, [B["kaw0"], qbuf, B["cstb"], biasbuf], [PB[sb2[0]], PB[sb2[1]]])
                    act(ptn[pj][:, 0:nb, :], psum[:, sb2[0]:sb2[0] + 2, :].rearrange("p b (j q) -> p (b j) q", q=128)[:, 0:nb, :],
                        AF.Exp, [PB[sb2[0]], PB[sb2[1]]], [B[f"ptn{pj}"]], scale=0.125)

                def na_pv(h):
                    ch, hh = h // 2, h % 2
                    pj = h % 2
                    c0 = ch * 192 + (0 if hh == 0 else 64)
                    calls = []
                    for j, kb in enumerate(kbs):
                        lb = kb - wlo
                        calls.append((psum[:, 4, h * 128:(h + 1) * 128], vaw[slot][:, lb, c0:c0 + 128], ptn[pj][:, j, :], j == 0, j == nb - 1, None))
                    mm(calls, [B["vaw0"], B[f"ptn{pj}"]], [PB[4]])
                na_qk(0)
                for h in range(4):
                    if h + 1 < 4:
                        na_qk(h + 1)
                    na_pv(h)
                for h in range(4):
                    ch, hh = h // 2, h % 2
                    num = slice(0, 64) if hh == 0 else slice(64, 128)
                    den = slice(64, 128) if hh == 0 else slice(0, 64)
                    dve(lambda e, h=h, num=num, den=den: e.reciprocal(rc[0][num, h * 128:(h + 1) * 128], psum[den, 4, h * 128:(h + 1) * 128]),
                        [PB[4]], [B["rc0"]])
                    dve(lambda e, h=h, num=num, ch=ch, s=s: e.tensor_tensor(ao[num, ch, s * 128:(s + 1) * 128], psum[num, 4, h * 128:(h + 1) * 128],
                                                                            rc[0][num, h * 128:(h + 1) * 128], ALU.mult),
                        [PB[4], B["rc0"]], [B["ao"]])

        def gqa_tile(l, t, slot, filler=None):
            q = qo[slot]
            qbuf = B[f"qo{slot}"]
            NKC = SEQ // 128
            for c in range(4):
                o4, o5 = (4, 5)
                if filler is not None:
                    next(filler, None)

                def qk(kc):
                    sb2 = (0, 1) if kc % 2 == 0 else (2, 3)
                    mm([(psum[:, sb2[0], :], kcr[0:64, kc * 128:(kc + 1) * 128], q[0:64, 4 + c, :], True, True, (0, 0)),
                        (psum[:, sb2[1], :], kcr[64:128, kc * 128:(kc + 1) * 128], q[64:128, 4 + c, :], True, True, (64, 0))],
                       [B["kcr"], qbuf], [PB[sb2[0]], PB[sb2[1]]])
                    pi = kc % 3
                    act(pt[pi][:], psum[:, sb2[0]:sb2[0] + 2, :], AF.Exp, [PB[sb2[0]], PB[sb2[1]]], [B[f"pt{pi}"]], scale=0.125)

                def pv(kc):
                    pi = kc % 3
                    mm([(psum[:, o4, :], vcr[:, kc, 0:128], pt[pi][:, 0, :], kc == 0, kc == NKC - 1, None),
                        (psum[:, o5, :], vcr[:, kc, 64:192], pt[pi][:, 1, :], kc == 0, kc == NKC - 1, None)],
                       [B["vcr"], B[f"pt{pi}"]], [PB[o4], PB[o5]])
                qk(0)
                qk(1)
                for kc in range(NKC):
                    if kc + 2 < NKC:
                        qk(kc + 2)
                    pv(kc)
                    if filler is not None and kc % 7 == 6:
                        next(filler, None)
                j = c % 2
                dve(lambda e, j=j, o4=o4: e.reciprocal(rc[j][0:64, :], psum[64:128, o4, :]), [PB[o4]], [B[f"rc{j}"]])
                dve(lambda e, c=c, j=j, o4=o4: e.tensor_tensor(ao[0:64, 2 + c, :], psum[0:64, o4, :], rc[j][0:64, :], ALU.mult), [PB[o4], B[f"rc{j}"]], [B["ao"]])
                dve(lambda e, j=j, o5=o5: e.reciprocal(rc[j][64:128, :], psum[0:64, o5, :]), [PB[o5]], [B[f"rc{j}"]])
                dve(lambda e, c=c, j=j, o5=o5: e.tensor_tensor(ao[64:128, 2 + c, :], psum[64:128, o5, :], rc[j][64:128, :], ALU.mult), [PB[o5], B[f"rc{j}"]], [B["ao"]])

        pend = {"gen": None, "fin": None}

        def drain_ffn2():
            if pend["gen"] is not None:
                for _ in pend["gen"]:
                    pass
                pend["fin"]()
                pend["gen"] = None

        def pass_b_tile(l, t, last):
            gi = l % 2
            slot = t % 2
            q, qbuf = qo[slot], B[f"qo{slot}"]
            xt, xbuf = xs[slot], B[f"xs{slot}"]
            S.dma("sp", f"qo{slot}", lambda e: [e.dma_start(out=q[:], in_=QO[:, :, t * T:(t + 1) * T])], 1,
                  reads=[B["d_QO"]], writes=[qbuf])
            lo = max(0, 4 * t - 2)
            hi = min(64, 4 * t + 6)
            off = lo - (4 * t - 2)
            S.dma("sp", "naw", lambda e: [e.dma_start(out=kaw[slot][:, :, off * 128:(off + hi - lo) * 128], in_=KA[:, :, lo * 128:hi * 128]),
                                          e.dma_start(out=vaw[slot][:, off:off + hi - lo, :], in_=VA[:, lo:hi, :])], 2,
                  reads=[B["d_KA"], B["d_VA"]], writes=[B["kaw0"], B["vaw0"]])
            load_x(t, slot, X1, B["d_X1"])
            na_tile(l, t, slot)
            gqa_tile(l, t, slot, filler=pend["gen"])
            drain_ffn2()
            rhs_k = [(ao[:, 0, :], B["ao"]), (ao[:, 1, :], B["ao"]), (q[:, 2, :], qbuf), (q[:, 3, :], qbuf)] + \
                    [(ao[:, 2 + c, :], B["ao"]) for c in range(4)]
            for g in range(4):
                wv, wvb = wload("wo", l, wsrc("wo", l, 0, 8, g * 256, 256), 8, 256)
                for nl in range(2):
                    n = g * 2 + nl
                    zb = n % 4
                    mm([(psum[:, zb, :], wv[:, k, nl * 128:(nl + 1) * 128], rhs_k[k][0], k == 0, k == 7, None) for k in range(8)],
                       [wvb, B["ao"], qbuf], [PB[zb]])
                    dve(lambda e, n=n, zb=zb: e.tensor_tensor(xt[:, n, :], xt[:, n, :], psum[:, zb, :], ALU.add), [PB[zb], xbuf], [xbuf])
            rmsnorm_x(xt, xbuf, 16, gi)
            pend["gen"] = ffn_gen(l, "wg2", "wu2", "wd2", xt, xbuf, gu_banks=((6, 7),), d_banks=(6, 7), tanh_silu=True)
            if last:
                pend["fin"] = lambda: store_x_tokmajor(t, slot)
            else:
                pend["fin"] = lambda: store_x(t, slot, X0, B["d_X0"])

        S.dma("sp", "cst", lambda e: [e.dma_start(out=cst32[:], in_=consts_d[:])], 1, writes=[B["cst32"]])
        dve(lambda e: e.tensor_copy(cstb[:], cst32[:]), [B["cst32"]], [B["cstb"]])
        dve(lambda e: e.memset(epsb[:], EPS), [], [B["epsb"]])
        def init_ones():
            for i in range(2):
                dve(lambda e, i=i: e.memset(vast[i][:], 1.0), [], [B[f"vast{i}"]])
                dve(lambda e, i=i: e.memset(vcst[i][:], 1.0), [], [B[f"vcst{i}"]])
        queue_casts(0, KA_)
        pump_casts(10 ** 9)
        queue_casts(0, KB_)
        for l in range(L):
            gi = l % 2
            if l > 0:
                phase_switch("B", "A")
            init_ones()
            layer_params(l)
            for t in range(ntiles):
                pass_a_tile(l, t, first=(l == 0))
                pump_casts(2)
            if stop in ("ffn1", "a", "a1", "a2", "a3"):
                break
            pump_casts(10 ** 9)
            phase_switch("A", "B")
            S.dma("sp", "nabr", lambda e, l=l, gi=gi: [e.dma_start(out=nabr[gi][:], in_=WB["nab", l][0])], 1,
                  reads=[DB("nab", l)], writes=[B["nabr0"]])
            S.dma("sp", "kvr", lambda e: [e.dma_start(out=kcr[:], in_=KC[:]), e.dma_start(out=vcr[:], in_=VC[:])], 2,
                  reads=[B["d_KC"], B["d_VC"]], writes=[B["kcr"], B["vcr"]])
            if l + 1 < L:
                queue_casts(l + 1, KA_ + KB_)
            for t in range(ntiles if ntiles_b is None else ntiles_b):
                pass_b_tile(l, t, last=(l == L - 1))
                pump_casts(4)
            drain_ffn2()
            pump_casts(10 ** 9)
        outs = [B["d_y"], B["d_X1"], B["d_X0"], B["d_QO"], B["d_KA"], B["d_KC"], B["d_VA"], B["d_VC"]]
        S.final_wait("pool", outs)
        S.final_wait("sp", outs)
        S.emit_all()
        S.close()
    return nc


_CACHE = {}


def kernel(**inputs):
    sh = prep_shared(inputs)
    xp = np.asarray(inputs["x_prompt"], np.float32)
    xsm = np.asarray(inputs["x_sample"], np.float32)
    seqs = [xp[i] for i in range(xp.shape[0])] + [xsm[i] for i in range(xsm.shape[0])]
    assert len(seqs) == NCORES
    if "nc" not in _CACHE:
        _CACHE["nc"] = build()
    nc = _CACHE["nc"]
    in_maps = []
    for s in seqs:
        m = dict(sh)
        m["x"] = np.ascontiguousarray(s)
        in_maps.append(m)
    res = run_bass_kernel_spmd(nc, in_maps, core_ids=list(range(NCORES)))
    ys = [np.asarray(r["y"], np.float32) for r in res.results]
    y_prompt = np.stack(ys[:xp.shape[0]], axis=0)
    y_sample = np.stack(ys[xp.shape[0]:], axis=0)
    return (y_prompt, y_sample)
```
